# Optimizing a Trainium2 kernel written in Bass

```python
import jax, jax.numpy as jnp
from jax import lax
import numpy as np

D_MODEL = 1024
BATCH = 2
SEQ = 8192
DEPTH = 2
DEC_BATCH = 16
DEC_SEQ = 16
PAST_LEN = 4096

CHUNK = 64
MIX_WIDTH = D_MODEL
DN_WIDTH = MIX_WIDTH // 2
DN_HEADS = 4
DN_HEAD_DIM = DN_WIDTH // DN_HEADS
CONV_WIDTH = 4
CONV_CH = 3 * DN_WIDTH
POOL_WIDTH = MIX_WIDTH - DN_WIDTH
POOL_WINDOWS = (2, 4, 8, 16)
POOL_GROUPS = len(POOL_WINDOWS)
POOL_GROUP_DIM = POOL_WIDTH // POOL_GROUPS
POOL_HIST = max(POOL_WINDOWS) - 1
D_FF = 2816
N_IN = 4 * DN_WIDTH + 2 * DN_HEADS + POOL_WIDTH
EPS = 1e-6

kernel_name = "hybrid_deltanet_pool_macaron_step"


def rms_norm(x, w):
    x32 = x.astype(jnp.float32)
    y = x32 * lax.rsqrt(jnp.mean(x32 * x32, axis=-1, keepdims=True) + EPS)
    return (y * w.astype(jnp.float32)).astype(x.dtype)


def l2_norm(x):
    x32 = x.astype(jnp.float32)
    return x32 * lax.rsqrt(jnp.sum(x32 * x32, axis=-1, keepdims=True) + EPS)


def swiglu(x, w_gate, w_up, w_down):
    return (jax.nn.silu(x @ w_gate) * (x @ w_up)) @ w_down


def causal_dwconv(x, buf, w):
    L = x.shape[1]
    xpad = jnp.concatenate([buf.astype(x.dtype), x], axis=1)
    y = xpad[:, 0:L] * w[0]
    for j in range(1, CONV_WIDTH):
        y = y + xpad[:, j:j + L] * w[j]
    return y, xpad[:, -(CONV_WIDTH - 1):]


def gated_delta_rule(q, k, v, g, beta, s0, chunk):
    B, L, H, dk = q.shape
    dv = v.shape[-1]
    n = L // chunk
    f32 = jnp.float32

    def blocks(t):
        t = t.astype(f32).reshape((B, n, chunk) + t.shape[2:])
        return jnp.moveaxis(t, 2, 3)

    q, k, v, g, beta = blocks(q), blocks(k), blocks(v), blocks(g), blocks(beta)
    gc = jnp.cumsum(g, axis=-1)
    idx = jnp.arange(chunk)
    incl = idx[:, None] >= idx[None, :]
    strict = idx[:, None] > idx[None, :]
    decay = jnp.exp(jnp.where(incl, gc[..., :, None] - gc[..., None, :], -jnp.inf))
    kk = jnp.einsum('bnhid,bnhjd->bnhij', k, k)
    a_mat = jnp.where(strict, beta[..., :, None] * kk * decay, 0.0) + jnp.eye(chunk, dtype=f32)
    rhs = jnp.concatenate([v * beta[..., None], k * (beta * jnp.exp(gc))[..., None]], axis=-1)
    sol = lax.linalg.triangular_solve(a_mat, rhs, left_side=True, lower=True, unit_diagonal=True)
    u, w = sol[..., :dv], sol[..., dv:]
    qk = jnp.einsum('bnhid,bnhjd->bnhij', q, k) * decay
    g_last = gc[..., -1]
    q_dec = q * jnp.exp(gc)[..., None]
    k_dec = k * jnp.exp(g_last[..., None] - gc)[..., None]

    def step(S, xs):
        q_c, qk_c, u_c, w_c, k_c, gl = xs
        v_new = u_c - jnp.einsum('bhcd,bhde->bhce', w_c, S)
        o_c = jnp.einsum('bhcd,bhde->bhce', q_c, S) + jnp.einsum('bhij,bhje->bhie', qk_c, v_new)
        S = S * jnp.exp(gl)[..., None, None] + jnp.einsum('bhcd,bhce->bhde', k_c, v_new)
        return S, o_c

    xs = (jnp.moveaxis(q_dec, 1, 0), jnp.moveaxis(qk, 1, 0), jnp.moveaxis(u, 1, 0),
          jnp.moveaxis(w, 1, 0), jnp.moveaxis(k_dec, 1, 0), jnp.moveaxis(g_last, 1, 0))
    S, o = lax.scan(step, s0.astype(f32), xs)
    o = jnp.transpose(o, (1, 0, 3, 2, 4)).reshape(B, L, H, dv)
    return o, S


def multi_scale_pool(u, buf, w_pool, scale, pos0):
    B, L, _ = u.shape
    upad = jnp.concatenate([buf.astype(u.dtype), u], axis=1)
    cs = jnp.cumsum(upad.astype(jnp.float32), axis=1)
    cs = jnp.concatenate([jnp.zeros((B, 1, POOL_WIDTH), jnp.float32), cs], axis=1)
    pos = pos0 + jnp.arange(L)
    means = []
    for gi, win in enumerate(POOL_WINDOWS):
        sl = slice(gi * POOL_GROUP_DIM, (gi + 1) * POOL_GROUP_DIM)
        s = cs[:, POOL_HIST + 1:POOL_HIST + 1 + L, sl] - cs[:, POOL_HIST + 1 - win:POOL_HIST + 1 - win + L, sl]
        cnt = jnp.minimum(win, pos + 1).astype(jnp.float32)
        means.append(s / cnt[None, :, None])
    d = (jnp.concatenate(means, axis=-1) - u.astype(jnp.float32)).reshape(B, L, POOL_GROUPS, POOL_GROUP_DIM)
    z = jnp.einsum('blgc,gcd->blgd', d, w_pool.astype(jnp.float32)).reshape(B, L, POOL_WIDTH)
    z = z * scale.astype(jnp.float32)
    return z.astype(u.dtype), upad[:, -POOL_HIST:]


def mixer(h, w_in, conv_w, a_log, dt_bias, o_norm, w_pool, pool_scale, w_out,
          s0, conv_buf, pool_buf, pos0, chunk):
    B, L, _ = h.shape
    p = h @ w_in
    c1, c2, c3 = 3 * DN_WIDTH, 4 * DN_WIDTH, 4 * DN_WIDTH + DN_HEADS
    c4 = c3 + DN_HEADS
    qkv, gate, a, b, u = p[..., :c1], p[..., c1:c2], p[..., c2:c3], p[..., c3:c4], p[..., c4:]
    qkv, conv_new = causal_dwconv(qkv, conv_buf, conv_w)
    qkv = jax.nn.silu(qkv)
    q = qkv[..., :DN_WIDTH].reshape(B, L, DN_HEADS, DN_HEAD_DIM)
    k = qkv[..., DN_WIDTH:2 * DN_WIDTH].reshape(B, L, DN_HEADS, DN_HEAD_DIM)
    v = qkv[..., 2 * DN_WIDTH:].reshape(B, L, DN_HEADS, DN_HEAD_DIM)
    q = l2_norm(q) * (DN_HEAD_DIM ** -0.5)
    k = l2_norm(k)
    beta = jax.nn.sigmoid(b.astype(jnp.float32))
    g = -jnp.exp(a_log.astype(jnp.float32)) * jax.nn.softplus(a.astype(jnp.float32) + dt_bias.astype(jnp.float32))
    o, s_new = gated_delta_rule(q, k, v, g, beta, s0, chunk)
    o = rms_norm(o.astype(h.dtype), o_norm) * jax.nn.silu(gate.reshape(B, L, DN_HEADS, DN_HEAD_DIM))
    o = o.reshape(B, L, DN_WIDTH)
    z, pool_new = multi_scale_pool(u, pool_buf, w_pool, pool_scale, pos0)
    out = jnp.concatenate([o, z.astype(o.dtype)], axis=-1) @ w_out
    return out, s_new.astype(s0.dtype), conv_new, pool_new


def setup_inputs(seed: int = 0) -> dict:
    key = jax.random.key(seed)
    ks = jax.random.split(key, 24)
    f32 = jnp.float32
    nrm = lambda k, shape, s: jax.random.normal(k, shape, f32) * s
    dt = jnp.exp(jax.random.uniform(ks[10], (DEPTH, DN_HEADS), f32, np.log(1e-3), np.log(1e-1)))
    return {
        "x_prompt": nrm(ks[0], (BATCH, SEQ, D_MODEL), 1.0),
        "x_sample": nrm(ks[1], (DEC_BATCH, DEC_SEQ, D_MODEL), 1.0),
        "state_delta": nrm(ks[2], (DEPTH, DEC_BATCH, DN_HEADS, DN_HEAD_DIM, DN_HEAD_DIM), 0.05),
        "state_conv": nrm(ks[3], (DEPTH, DEC_BATCH, CONV_WIDTH - 1, CONV_CH), 1.0),
        "state_pool": nrm(ks[4], (DEPTH, DEC_BATCH, POOL_HIST, POOL_WIDTH), 1.0),
        "norm_ffn1": 1.0 + nrm(ks[5], (DEPTH, D_MODEL), 0.02),
        "w_ffn1_gate": nrm(ks[6], (DEPTH, D_MODEL, D_FF), D_MODEL ** -0.5),
        "w_ffn1_up": nrm(ks[7], (DEPTH, D_MODEL, D_FF), D_MODEL ** -0.5),
        "w_ffn1_down": nrm(ks[8], (DEPTH, D_FF, D_MODEL), D_FF ** -0.5),
        "norm_mix": 1.0 + nrm(ks[9], (DEPTH, D_MODEL), 0.02),
        "w_in": nrm(ks[11], (DEPTH, D_MODEL, N_IN), D_MODEL ** -0.5),
        "conv_w": nrm(ks[12], (DEPTH, CONV_WIDTH, CONV_CH), CONV_WIDTH ** -0.5),
        "a_log": jnp.log(jax.random.uniform(ks[13], (DEPTH, DN_HEADS), f32, 1.0, 16.0)),
        "dt_bias": dt + jnp.log(-jnp.expm1(-dt)),
        "o_norm": 1.0 + nrm(ks[14], (DEPTH, DN_HEAD_DIM), 0.02),
        "w_pool": nrm(ks[15], (DEPTH, POOL_GROUPS, POOL_GROUP_DIM, POOL_GROUP_DIM), POOL_GROUP_DIM ** -0.5),
        "pool_scale": 1.0 + nrm(ks[16], (DEPTH, POOL_WIDTH), 0.02),
        "w_out": nrm(ks[17], (DEPTH, MIX_WIDTH, D_MODEL), MIX_WIDTH ** -0.5),
        "norm_ffn2": 1.0 + nrm(ks[18], (DEPTH, D_MODEL), 0.02),
        "w_ffn2_gate": nrm(ks[19], (DEPTH, D_MODEL, D_FF), D_MODEL ** -0.5),
        "w_ffn2_up": nrm(ks[20], (DEPTH, D_MODEL, D_FF), D_MODEL ** -0.5),
        "w_ffn2_down": nrm(ks[21], (DEPTH, D_FF, D_MODEL), D_FF ** -0.5),
        "norm_final": 1.0 + nrm(ks[22], (D_MODEL,), 0.02),
    }


def reference(x_prompt, x_sample, state_delta, state_conv, state_pool,
              norm_ffn1, w_ffn1_gate, w_ffn1_up, w_ffn1_down, norm_mix, w_in, conv_w,
              a_log, dt_bias, o_norm, w_pool, pool_scale, w_out,
              norm_ffn2, w_ffn2_gate, w_ffn2_up, w_ffn2_down, norm_final):
    dt = x_prompt.dtype

    def layer(x, l, s0, conv_buf, pool_buf, pos0, chunk):
        x = x + 0.5 * swiglu(rms_norm(x, norm_ffn1[l]), w_ffn1_gate[l], w_ffn1_up[l], w_ffn1_down[l])
        m, s_new, conv_new, pool_new = mixer(rms_norm(x, norm_mix[l]), w_in[l], conv_w[l], a_log[l],
                                             dt_bias[l], o_norm[l], w_pool[l], pool_scale[l], w_out[l],
                                             s0, conv_buf, pool_buf, pos0, chunk)
        x = x + m
        x = x + 0.5 * swiglu(rms_norm(x, norm_ffn2[l]), w_ffn2_gate[l], w_ffn2_up[l], w_ffn2_down[l])
        return x, s_new, conv_new, pool_new

    xp = x_prompt
    p_delta, p_conv, p_pool = [], [], []
    for l in range(DEPTH):
        xp, s_new, c_new, q_new = layer(
            xp, l,
            jnp.zeros((BATCH, DN_HEADS, DN_HEAD_DIM, DN_HEAD_DIM), dt),
            jnp.zeros((BATCH, CONV_WIDTH - 1, CONV_CH), dt),
            jnp.zeros((BATCH, POOL_HIST, POOL_WIDTH), dt),
            0, CHUNK)
        p_delta.append(s_new); p_conv.append(c_new); p_pool.append(q_new)
    y_prompt = rms_norm(xp, norm_final)

    xs = x_sample
    s_delta, s_conv, s_pool = [], [], []
    for l in range(DEPTH):
        xs, s_new, c_new, q_new = layer(xs, l, state_delta[l], state_conv[l], state_pool[l],
                                        PAST_LEN, xs.shape[1])
        s_delta.append(s_new); s_conv.append(c_new); s_pool.append(q_new)
    y_sample = rms_norm(xs, norm_final)

    return (y_prompt, y_sample,
            jnp.stack(p_delta), jnp.stack(p_conv), jnp.stack(p_pool),
            jnp.stack(s_delta), jnp.stack(s_conv), jnp.stack(s_pool))
```

```python
import numpy as np
import concourse.bass as bass
import concourse.mybir as mybir
from concourse.bass_utils import run_bass_kernel_spmd
from contextlib import ExitStack

F32 = mybir.dt.float32
F32R = mybir.dt.float32r
BF16 = mybir.dt.bfloat16
AF = mybir.ActivationFunctionType
ALU = mybir.AluOpType

NCORES = 8
D = 1024
KC = 8
DFF = 2816
FC = 22
NP = 2048
NS = 32
NT = NP + NS
DEPTH = 2
EPS = 1e-6
TT = [(0, 512), (512, 512), (1024, 512), (1536, 512), (2048, 32)]
EPOCH = 3000


class Reg:
    __slots__ = ("aid", "lo", "hi", "ap", "bank")

    def __init__(self, aid, lo, hi, ap):
        self.aid, self.lo, self.hi, self.ap = aid, lo, hi, ap


class Arena:
    def __init__(self, name, handle_ap, nbytes):
        self.name, self.base, self.nbytes = name, handle_ap, nbytes

    def view(self, lo, shape, dtype, parts=128, p0=0):
        esz = 2 if dtype == BF16 else 4
        n = 1
        for s in shape:
            n *= s
        hi = lo + n * esz
        assert lo % 4 == 0 and hi <= self.nbytes, (self.name, lo, hi, self.nbytes)
        ap = self.base[p0:p0 + parts, lo // 4:(hi + 3) // 4]
        if dtype == BF16:
            ap = ap.bitcast(dtype)[:, 0:n]
        elif dtype == F32R:
            ap = ap.bitcast(dtype)
        if len(shape) == 2:
            ap = ap.rearrange("p (a b) -> p a b", b=shape[1])
        elif len(shape) == 3:
            ap = ap.rearrange("p (a b c) -> p a b c", b=shape[1], c=shape[2])
        return Reg(self.name, lo, hi, ap)


class Prog:
    ENGS = ("pe", "act", "dve", "pool", "sp")

    def __init__(self):
        self.streams = {e: [] for e in self.ENGS}
        self.count = {e: 0 for e in self.ENGS}
        self.records = {}
        self.waited = {e: {} for e in self.ENGS}
        self.semkeys = {}
        self.dma_total = {}

    def _deps(self, eng, reads, writes, is_dma):
        deps = {}

        def add(tok):
            k, v = tok[0], tok[1]
            if deps.get(k, 0) < v:
                deps[k] = v

        for r in reads:
            for rec in self.records.get(r.aid, ()):
                if rec[3] and rec[0] < r.hi and r.lo < rec[1]:
                    add(rec[2])
        for w in writes:
            for rec in self.records.get(w.aid, ()):
                if rec[0] < w.hi and w.lo < rec[1]:
                    if rec[4] == eng and eng == "pe" and not is_dma and not rec[5]:
                        continue
                    add(rec[2])
        return deps

    def _record(self, eng, reads, writes, tok, is_dma):
        for w in writes:
            lst = self.records.setdefault(w.aid, [])
            lst[:] = [r for r in lst if not (w.lo <= r[0] and r[1] <= w.hi)]
            lst.append([w.lo, w.hi, tok, True, eng, is_dma])
        for r in reads:
            lst = self.records.setdefault(r.aid, [])
            if not is_dma:
                lst[:] = [x for x in lst if not (not x[3] and x[4] == eng and not x[5]
                                                 and r.lo <= x[0] and x[1] <= r.hi)]
            lst.append([r.lo, r.hi, tok, False, eng, is_dma])

    def _emit(self, eng, deps, fn, inc):
        waits = []
        wd = self.waited[eng]
        for k, v in deps.items():
            if wd.get(k, 0) < v:
                wd[k] = v
                waits.append((k, v))
        self.streams[eng].append((waits, fn, inc))

    def op(self, eng, fn, reads=(), writes=()):
        deps = self._deps(eng, reads, writes, False)
        self.count[eng] += 1
        n = self.count[eng]
        key = ("e", eng, (n - 1) // EPOCH)
        tok = (key, (n - 1) % EPOCH + 1)
        self.semkeys[key] = True
        self._emit(eng, deps, fn, (key, 1))
        self._record(eng, reads, writes, tok, False)

    def dma(self, eng, slot, fn, reads=(), writes=(), n=1, inc=16):
        deps = self._deps(eng, reads, writes, True)
        key = ("d", slot)
        self.semkeys[key] = True
        tot = self.dma_total.get(key, 0) + inc * n
        self.dma_total[key] = tot
        tok = (key, tot)
        self._emit(eng, deps, fn, (key, inc))
        self._record(eng, reads, writes, tok, True)
        return tok

    def wait_all_dma(self, eng):
        deps = dict(self.dma_total)
        self._emit(eng, deps, None, None)


HW = 16
NBLK = NP // 128


def build_nc(cfg=None):
    cfg = cfg or {}
    NL = cfg.get("nl", DEPTH)
    DBG = cfg.get("dbg", None)
    nc = bass.Bass("TRN2", target_bir_lowering=False)
    P = Prog()

    def din(name, shape):
        return nc.dram_tensor(name, list(shape), F32, kind="ExternalInput").ap()

    def dout(name, shape):
        return nc.dram_tensor(name, list(shape), F32, kind="ExternalOutput").ap()

    xT_d = din("xT", [KC, 128, NT])
    nrm_d = din("nrm", [128, DEPTH * 3 + 1, KC])
    wg_d = [[din(f"wg{l}{f}", [FC, 128, KC, 128]) for f in range(2)] for l in range(DEPTH)]
    wu_d = [[din(f"wu{l}{f}", [FC, 128, KC, 128]) for f in range(2)] for l in range(DEPTH)]
    wd_d = [[din(f"wd{l}{f}", [2, KC, 128, 11, 128]) for f in range(2)] for l in range(DEPTH)]
    wqkv_d = [din(f"wqkv{l}", [12, 128, KC, 128]) for l in range(DEPTH)]
    wgate_d = [din(f"wgate{l}", [4, 128, KC, 128]) for l in range(DEPTH)]
    wpu_d = [din(f"wpu{l}", [4, 128, KC, 128]) for l in range(DEPTH)]
    wab_d = [din(f"wab{l}", [128, KC, 8]) for l in range(DEPTH)]
    wout_d = [din(f"wout{l}", [KC, 128, KC, 128]) for l in range(DEPTH)]
    wpool_d = [din(f"wpool{l}", [4, 128, 128]) for l in range(DEPTH)]
    small_d = din("small", [128, DEPTH, 72])
    consts_d = din("consts", [128, 5, 128])
    sdelta_d = din("sdelta", [DEPTH, 2, 4, 128, 128])
    sconv_d = din("sconv", [128, DEPTH, 2, 12, 3])
    spool_d = din("spool", [128, DEPTH, 2, 4, HW])
    percore_d = din("percore", [128, 8 + 64])
    yT_d = dout("yT", [KC, 128, NT])
    o_pdelta = dout("o_pdelta", [DEPTH, 4, 128, 128])
    o_halo = dout("o_halo", [DEPTH, 128, 16, HW])
    o_sdelta = dout("o_sdelta", [DEPTH, 2, 4, 128, 128])
    o_sconv = dout("o_sconv", [DEPTH, 2, 128, 12, 3])
    o_spool = dout("o_spool", [DEPTH, 2, 128, 4, HW])
    dbg_d = dout("dbg", [128, 4096]) if DBG else None
    XCH = not cfg.get("noxch")
    cca_in = [nc.dram_tensor(f"cca_in{l}", [128, 256], F32) for l in range(DEPTH)]
    cca_out = [nc.dram_tensor(f"cca_out{l}", [4 * 128, 256], F32) for l in range(DEPTH)]
    ccb_in = [nc.dram_tensor(f"ccb_in{l}", [128, 1024], F32) for l in range(DEPTH)]
    ccb_out = [nc.dram_tensor(f"ccb_out{l}", [4 * 128, 1024], F32) for l in range(DEPTH)]
    GROUPS = [[0, 1, 2, 3], [4, 5, 6, 7]]

    with ExitStack() as es:
        SB_BYTES = cfg.get("sb", 201) * 1024 + 512
        SBR_BYTES = 3 * 4 * 128 * 4
        sbr_t = es.enter_context(nc.sbuf_tensor("arena_r", [128, SBR_BYTES // 4], F32))
        sb_t = es.enter_context(nc.sbuf_tensor("arena", [128, SB_BYTES // 4], F32))
        ps_t = es.enter_context(nc.psum_tensor("psum", [128, 8 * 512], F32))
        sb = Arena("sb", sb_t[:, :], SB_BYTES)
        sbr = Arena("sbr", sbr_t[:, :], SBR_BYTES)
        roff = [0]
        ps = Arena("ps", ps_t[:, :], 8 * 2048)

        off = [0]

        def alloc(nbytes):
            lo = off[0]
            off[0] += (nbytes + 31) // 32 * 32
            assert off[0] <= SB_BYTES, (off[0], SB_BYTES)
            return lo

        XT = alloc(KC * NT * 4)
        XN = alloc(KC * NT * 2)
        NRM = alloc((DEPTH * 3 + 1) * KC * 4)
        SMALL = alloc(DEPTH * 72 * 4)
        PERC = alloc(72 * 4)
        CST = alloc(5 * 128 * 4)
        CSTR = alloc(2 * 128 * 4)
        ONES = alloc(128 * 2)
        EPSC = alloc(32)
        WAB = alloc(KC * 8 * 2)
        WX = alloc(KC * 128 * 2)
        NW = 6
        WG = [alloc(KC * 128 * 2) for _ in range(NW)]
        WU = [alloc(KC * 128 * 2) for _ in range(NW)]
        SH0 = off[0]
        HID = alloc(11 * NT * 2)
        RSTD = alloc(NT * 4)
        NWD = 3
        WD = [alloc(11 * 128 * 2) for _ in range(NWD)]
        NSG = 3
        SG = [alloc(512 * 2) for _ in range(NSG)]
        SH1 = SB_BYTES

        def xT(c, a, n):
            return sb.view(XT + (c * NT + a) * 4, [n], F32)

        def xn(c, a, n):
            return sb.view(XN + (c * NT + a) * 2, [n], BF16)

        def hid(j, a, n):
            return sb.view(HID + (j * NT + a) * 2, [n], BF16)

        def rstd(a, n):
            return sb.view(RSTD + a * 4, [n], F32)

        def nrmw(i, c):
            return sb.view(NRM + (i * KC + c) * 4, [1], F32)

        ones_bf = sb.view(ONES, [128], BF16)
        eps_c = sb.view(EPSC, [1], F32)
        nrm_all = sb.view(NRM, [(DEPTH * 3 + 1) * KC], F32)
        xT_all = sb.view(XT, [KC, NT], F32)
        small = sb.view(SMALL, [DEPTH, 72], F32)
        perc = sb.view(PERC, [72], F32)
        cst = sb.view(CST, [5, 128], F32)
        ones_r = sb.view(CSTR, [128], F32R)
        IDENT, TRIU, MU_S, ML_S, MU_I = range(5)

        bank_ctr = [0]

        held = set()

        def bank(n=512, parts=128, hold=False):
            while True:
                b = bank_ctr[0] % 8
                bank_ctr[0] += 1
                if b not in held:
                    break
            if hold:
                held.add(b)
            r = ps.view(b * 2048, [n], F32, parts=parts)
            r.bank = b
            return r

        dbg_off = [0]

        def dbg(name, reg, ap, parts, n):
            if DBG != name:
                return
            o = dbg_off[0]
            dbg_off[0] += n
            P.dma("sp", "dbg", lambda e: [e.dma_start(out=dbg_d[0:parts, o:o + n], in_=ap)], reads=[reg])

        P.dma("sp", "xin", lambda e: [e.dma_start(out=xT_all.ap, in_=xT_d.rearrange("c p t -> p c t"))],
              writes=[xT_all])
        P.dma("sp", "c0", lambda e: [e.dma_start(out=nrm_all.ap, in_=nrm_d.rearrange("p i c -> p (i c)"))],
              writes=[nrm_all])
        P.dma("sp", "c1", lambda e: [e.dma_start(out=small.ap, in_=small_d)], writes=[small])
        P.dma("sp", "c2", lambda e: [e.dma_start(out=perc.ap, in_=percore_d)], writes=[perc])
        P.dma("sp", "c3", lambda e: [e.dma_start(out=cst.ap, in_=consts_d)], writes=[cst])
        P.op("dve", lambda e: e.memset(ones_bf.ap, 1.0 / D), writes=[ones_bf])
        P.op("dve", lambda e: e.memset(eps_c.ap, EPS), writes=[eps_c])

        def rmsnorm(ni):
            for (a, n) in TT:
                pb = bank(n)
                sqs = []
                for c in range(KC):
                    s = hid(c, a, n)
                    x = xT(c, a, n)
                    P.op("act", lambda e, s=s, x=x: e.activation(out=s.ap, in_=x.ap, func=AF.Square),
                         reads=[x], writes=[s])
                    sqs.append(s)

                def mm(e, pb=pb, sqs=sqs):
                    ins = None
                    for c in range(KC):
                        ins = e.matmul(pb.ap, ones_bf.ap, sqs[c].ap, start=(c == 0), stop=(c == KC - 1))
                    return ins
                P.op("pe", mm, reads=[ones_bf] + sqs, writes=[pb])
                r = rstd(a, n)
                P.op("act", lambda e, r=r, pb=pb: e.activation(out=r.ap, in_=pb.ap, func=AF.Sqrt, bias=eps_c.ap),
                     reads=[pb, eps_c], writes=[r])
                P.op("dve", lambda e, r=r: e.reciprocal(out=r.ap, in_=r.ap), reads=[r], writes=[r])
                for c in range(KC):
                    x = xT(c, a, n)
                    o = xn(c, a, n)
                    w = nrmw(ni, c)
                    P.op("dve", lambda e, o=o, x=x, w=w, r=r: e.scalar_tensor_tensor(
                        out=o.ap, in0=x.ap, scalar=w.ap, in1=r.ap, op0=ALU.mult, op1=ALU.mult),
                        reads=[x, w, r], writes=[o])

        wctr = [0]
        wdctr = [0]
        sgctr = [0]

        def wtile(src_ap):
            slot = wctr[0] % (2 * NW)
            wctr[0] += 1
            w = sb.view((WG + WU)[slot], [KC, 128], BF16)
            P.dma("pool", f"w{slot}", lambda e: [e.dma_start(out=w.ap, in_=src_ap)], writes=[w])
            return w

        def proj(w, xs_fn, a, n, pb=None):
            pb = pb or bank(n)
            xs = [xs_fn(c, a, n) for c in range(KC)]

            def mm(e):
                ins = None
                for c in range(KC):
                    ins = e.matmul(pb.ap, w.ap[:, c, :], xs[c].ap, start=(c == 0), stop=(c == KC - 1))
                return ins
            P.op("pe", mm, reads=[w] + xs, writes=[pb])
            return pb

        def ffn(l, f):
            rmsnorm(l * 3 + (0 if f == 0 else 2))
            for half in range(2):
                for jj in range(11):
                    j = half * 11 + jj
                    wg = wtile(wg_d[l][f][j])
                    wu = wtile(wu_d[l][f][j])
                    for (a, n) in TT:
                        pg = proj(wg, xn, a, n)
                        pu = proj(wu, xn, a, n)
                        sg = sb.view(SG[sgctr[0] % NSG], [n], BF16)
                        sgctr[0] += 1
                        P.op("act", lambda e, sg=sg, pg=pg: e.activation(out=sg.ap, in_=pg.ap, func=AF.Silu),
                             reads=[pg], writes=[sg])
                        h = hid(jj, a, n)
                        P.op("dve", lambda e, h=h, sg=sg, pu=pu: e.tensor_tensor(out=h.ap, in0=sg.ap, in1=pu.ap,
                                                                                op=ALU.mult),
                             reads=[sg, pu], writes=[h])
                for dc in range(KC):
                    slot = wdctr[0] % NWD
                    wdctr[0] += 1
                    wd = sb.view(WD[slot], [11, 128], BF16)
                    P.dma("pool", f"wd{slot}", lambda e, wd=wd, dc=dc, half=half: [
                        e.dma_start(out=wd.ap, in_=wd_d[l][f][half, dc])], writes=[wd])
                    for (a, n) in TT:
                        pb = bank(n)
                        hs = [hid(jj, a, n) for jj in range(11)]

                        def mm(e, pb=pb, wd=wd, hs=hs):
                            ins = None
                            for jj in range(11):
                                ins = e.matmul(pb.ap, wd.ap[:, jj, :], hs[jj].ap, start=(jj == 0), stop=(jj == 10))
                            return ins
                        P.op("pe", mm, reads=[wd] + hs, writes=[pb])
                        x = xT(dc, a, n)
                        P.op("dve", lambda e, x=x, pb=pb: e.scalar_tensor_tensor(
                            out=x.ap, in0=pb.ap, scalar=0.5, in1=x.ap, op0=ALU.mult, op1=ALU.add),
                            reads=[pb, x], writes=[x])

        moff = [SH0]

        def malloc(nbytes):
            lo = moff[0]
            moff[0] += (nbytes + 31) // 32 * 32
            assert moff[0] <= SH1, (moff[0], SH1)
            return lo

        class MB:
            def __init__(self, shape, parts=128, dtype=F32, rr=False):
                n = 1
                for s_ in shape:
                    n *= s_
                self.shape, self.dtype = shape, dtype
                if rr:
                    self.lo = roff[0]
                    roff[0] += n * 4
                    self.reg = sbr.view(self.lo, shape, F32)
                    self.r = sbr.view(self.lo, shape, F32R).ap
                else:
                    self.lo = malloc(n * (2 if dtype == BF16 else 4))
                    self.reg = sb.view(self.lo, shape, dtype)
                    self.r = None
                self.ap = self.reg.ap

        OZT = MB([8, NT], dtype=BF16)
        COLS = MB([40])
        H16 = MB([16, HW])
        HSEL = MB([16, HW])
        HTMP = MB([16, HW])
        PRE = MB([4, 3 + 128])
        HALO = MB([12, 3])
        QKV = MB([12, 128])
        GCROW = MB([4, 128])
        DTb = MB([4, 128])
        DLb = MB([4, 128])
        Ab = MB([4, 128], dtype=BF16)
        Bb = MB([4, 128], dtype=BF16)
        Xb = MB([4, 128], dtype=BF16)
        QN = MB([8, 128], dtype=BF16)
        SQB = MB([8, 128], dtype=BF16)
        KBG = MB([4, 128], dtype=BF16)
        KDEC = MB([4, 128], dtype=BF16)
        VB = MB([4, 128], dtype=BF16)
        WNT = MB([4, 128], dtype=BF16)
        QDT = MB([4, 128], dtype=BF16)
        QKMT = MB([4, 128], dtype=BF16)
        VN = MB([4, 256], dtype=BF16)
        Zb = MB([4, 256])
        ZB = MB([4, 256], dtype=BF16)
        ONES1 = MB([128], dtype=BF16)
        post0 = PRE.lo
        UPW = HW + NP + 2 * (HW + 16)

        def mixer(l):
            cw = lambda ch, j: small.ap[:, l, ch * 4 + j: ch * 4 + j + 1]
            alog = small.ap[:, l, 48:52]
            dtb = small.ap[:, l, 52:56]
            onorm = small.ap[:, l, 56:57]
            pscale = lambda g: small.ap[:, l, 57 + g:58 + g]
            rmsnorm(l * 3 + 1)
            wq = []
            for ch in range(12):
                w = sb.view((WG + WU)[ch], [KC, 128], BF16)
                P.dma("pool", f"w{ch}", lambda e, w=w, ch=ch: [e.dma_start(out=w.ap, in_=wqkv_d[l][ch])], writes=[w])
                wq.append(w)
            wctr[0] = 0
            wab = sb.view(WAB, [KC, 8], BF16)
            P.dma("pool", "wab", lambda e: [e.dma_start(out=wab.ap, in_=wab_d[l])], writes=[wab])
            negA = COLS.ap[:, 32:36]
            P.op("act", lambda e: e.activation(out=negA, in_=alog, func=AF.Exp), reads=[small], writes=[COLS.reg])
            P.op("dve", lambda e: e.tensor_scalar(out=negA, in0=negA, scalar1=-1.0, scalar2=None, op0=ALU.mult),
                 reads=[COLS.reg], writes=[COLS.reg])
            one_c = COLS.ap[:, 36:37]
            P.op("dve", lambda e: e.memset(one_c, 1.0), writes=[COLS.reg])
            lnsc = COLS.ap[:, 37:38]
            P.op("dve", lambda e: e.memset(lnsc, -0.5 * float(np.log(128.0))), writes=[COLS.reg])

            hb = bank(16 * HW)
            wx = sb.view(WX, [KC, 128], BF16)
            for ch in range(16):
                if ch < 12:
                    wt = wq[ch]
                else:
                    wt = wx
                    P.dma("pool", "wx", lambda e, ch=ch: [e.dma_start(out=wx.ap, in_=wpu_d[l][ch - 12])], writes=[wx])
                xs = [xn(c, NP - HW, HW) for c in range(KC)]

                def mm(e, ch=ch, xs=xs, wt=wt):
                    ins = None
                    for c in range(KC):
                        ins = e.matmul(hb.ap[:, ch * HW:(ch + 1) * HW], wt.ap[:, c, :], xs[c].ap,
                                       start=(c == 0), stop=(c == KC - 1))
                    return ins
                P.op("pe", mm, reads=[wt] + xs, writes=[hb])
            P.op("act", lambda e: e.activation(out=H16.ap, in_=hb.ap.rearrange("p (c t) -> p c t", t=HW),
                                               func=AF.Copy), reads=[hb], writes=[H16.reg])
            P.dma("sp", "ohl", lambda e: [e.dma_start(out=o_halo[l], in_=H16.ap)], reads=[H16.reg])
            if XCH:
                ra_in = Reg(f"cca_in{l}", 0, 1, None)
                ra_out = Reg(f"cca_out{l}", 0, 1, None)
                P.dma("sp", "xa_st", lambda e: [e.dma_start(out=cca_in[l][:, :].rearrange("p (c t) -> p c t", t=HW), in_=H16.ap)],
                      reads=[H16.reg], writes=[ra_in])
                P.dma("pool", "xa_cc", lambda e: [e.collective_compute(
                    "AllGather", ALU.bypass, replica_groups=GROUPS, ins=[cca_in[l].ap().opt()], outs=[cca_out[l].ap().opt()])],
                    reads=[ra_in], writes=[ra_out], inc=1)
                for r in range(4):
                    P.dma("sp", "xa_ld", lambda e, r=r: [e.dma_start(
                        out=HTMP.ap, in_=cca_out[l][r * 128:(r + 1) * 128, :].rearrange("p (c t) -> p c t", t=HW))],
                        reads=[ra_out], writes=[HTMP.reg])
                    if r == 0:
                        P.op("dve", lambda e: e.tensor_scalar(out=HSEL.ap, in0=HTMP.ap, scalar1=perc.ap[:, 0:1], scalar2=None,
                                                              op0=ALU.mult), reads=[HTMP.reg, perc], writes=[HSEL.reg])
                    else:
                        P.op("dve", lambda e, r=r: e.scalar_tensor_tensor(out=HSEL.ap, in0=HTMP.ap, scalar=perc.ap[:, r:r + 1],
                                                                          in1=HSEL.ap, op0=ALU.mult, op1=ALU.add),
                             reads=[HTMP.reg, perc, HSEL.reg], writes=[HSEL.reg])
            else:
                P.op("dve", lambda e: e.memset(HSEL.ap, 0.0), writes=[HSEL.reg])
            STG = cfg.get('stage', 99)
            BST = cfg.get('bstage', 99)
            if STG < 1:
                return

            def s1_emit(a, C):
                pbs = []
                for s3 in range(3):
                    pb = bank(4 * C, hold=True)
                    xs = [xn(c, a, C) for c in range(KC)]

                    def mm(e, pb=pb, s3=s3, xs=xs):
                        ins = None
                        for h in range(4):
                            for c in range(KC):
                                ins = e.matmul(pb.ap[:, h * C:(h + 1) * C], wq[s3 * 4 + h].ap[:, c, :], xs[c].ap,
                                               start=(c == 0), stop=(c == KC - 1))
                        return ins
                    P.op("pe", mm, reads=wq[s3 * 4:s3 * 4 + 4] + xs, writes=[pb])
                    pbs.append(pb)
                pab = bank(8, parts=C, hold=True)
                xs = [xn(c, a, C) for c in range(KC)]

                def mm(e, xs=xs):
                    ins = None
                    for c in range(KC):
                        ins = e.matmul(pab.ap, xs[c].ap, wab.ap[:, c, :], start=(c == 0), stop=(c == KC - 1))
                    return ins
                P.op("pe", mm, reads=[wab] + xs, writes=[pab])
                return pbs, pab

            def block(a, C, ZW, SO, si, pre, next_s1=None):
                aug = ZW == 256
                idc = cst.ap[0:C, IDENT, 0:C]
                pbs, pab = pre
                nxt = None
                if BST < 6:
                    return
                cg = COLS.ap[0:C, 0:4]
                cb = COLS.ap[0:C, 4:8]
                cgc = COLS.ap[0:C, 8:12]
                cbg = COLS.ap[0:C, 12:16]
                ckd = COLS.ap[0:C, 16:20]
                ct = COLS.ap[0:C, 20:24]
                gam = COLS.ap[:, 24:28]
                P.op("dve", lambda e: e.tensor_tensor(out=ct, in0=pab.ap[:, 0:4], in1=dtb[0:C, :], op=ALU.add),
                     reads=[pab, small], writes=[COLS.reg])
                P.op("act", lambda e: e.activation(out=ct, in_=ct, func=AF.Exp), reads=[COLS.reg], writes=[COLS.reg])
                P.op("act", lambda e: e.activation(out=ct, in_=ct, func=AF.Ln, bias=one_c[0:C, :]),
                     reads=[COLS.reg], writes=[COLS.reg])
                P.op("dve", lambda e: e.tensor_tensor(out=cg, in0=ct, in1=negA[0:C, :], op=ALU.mult),
                     reads=[COLS.reg], writes=[COLS.reg])
                P.op("act", lambda e: e.activation(out=cb, in_=pab.ap[:, 4:8], func=AF.Exp, scale=-1.0),
                     reads=[pab], writes=[COLS.reg])
                held.discard(pab.bank)
                P.op("dve", lambda e: e.tensor_scalar(out=cb, in0=cb, scalar1=1.0, scalar2=None, op0=ALU.add),
                     reads=[COLS.reg], writes=[COLS.reg])
                P.op("dve", lambda e: e.reciprocal(out=cb, in_=cb), reads=[COLS.reg], writes=[COLS.reg])
                if BST < 7:
                    return
                pgc = bank(4, parts=C)
                P.op("pe", lambda e: e.matmul(pgc.ap, cst.ap[0:C, TRIU, 0:C], cg, start=True, stop=True),
                     reads=[cst, COLS.reg], writes=[pgc])
                P.op("dve", lambda e: e.tensor_copy(out=DLb.ap[0:C, :, :],
                                                    in_=cg.unsqueeze(2).broadcast_to([C, 4, 128])),
                     reads=[COLS.reg], writes=[DLb.reg])
                pgr = bank(4 * C)

                def mm(e):
                    ins = None
                    for h in range(4):
                        ins = e.matmul(pgr.ap[:, h * C:(h + 1) * C], DLb.ap[0:C, h, :], cst.ap[0:C, TRIU, 0:C],
                                       start=True, stop=True)
                    return ins
                P.op("pe", mm, reads=[cst, DLb.reg], writes=[pgr])
                P.op("dve", lambda e: e.tensor_copy(out=cgc, in_=pgc.ap), reads=[pgc], writes=[COLS.reg])
                P.op("act", lambda e: e.activation(out=GCROW.ap[:, :, 0:C], in_=pgr.ap.rearrange("p (h t) -> p h t", t=C),
                                                   func=AF.Copy), reads=[pgr], writes=[GCROW.reg])
                P.op("act", lambda e: e.activation(out=cbg, in_=cgc, func=AF.Exp), reads=[COLS.reg], writes=[COLS.reg])
                P.op("dve", lambda e: e.tensor_tensor(out=cbg, in0=cbg, in1=cb, op=ALU.mult),
                     reads=[COLS.reg], writes=[COLS.reg])
                P.op("dve", lambda e: e.tensor_tensor(out=ckd, in0=GCROW.ap[0:C, :, C - 1], in1=cgc, op=ALU.subtract),
                     reads=[GCROW.reg, COLS.reg], writes=[COLS.reg])
                P.op("act", lambda e: e.activation(out=ckd, in_=ckd, func=AF.Exp), reads=[COLS.reg], writes=[COLS.reg])
                P.op("act", lambda e: e.activation(out=gam, in_=GCROW.ap[:, :, C - 1], func=AF.Exp),
                     reads=[GCROW.reg], writes=[COLS.reg])
                P.op("act", lambda e: e.activation(out=COLS.ap[:, 38:39], in_=one_c, func=AF.Silu), reads=[COLS.reg], writes=[COLS.reg])
                if BST < 3:
                    return
                for s3 in range(3):
                    pb = pbs[s3]
                    P.op("act", lambda e, pb=pb: e.activation(out=PRE.ap[:, :, 3:3 + C],
                                                              in_=pb.ap.rearrange("p (h t) -> p h t", t=C),
                                                              func=AF.Copy), reads=[pb], writes=[PRE.reg])
                    held.discard(pb.bank)
                    P.op("dve", lambda e, s3=s3: e.tensor_copy(out=PRE.ap[:, :, 0:3], in_=HALO.ap[:, s3 * 4:s3 * 4 + 4, :]),
                         reads=[HALO.reg], writes=[PRE.reg])
                    P.op("dve", lambda e, s3=s3: e.tensor_copy(out=HALO.ap[:, s3 * 4:s3 * 4 + 4, :], in_=PRE.ap[:, :, C:C + 3]),
                         reads=[PRE.reg], writes=[HALO.reg])
                    eng = "dve"
                    tmpb = DTb
                    acc4 = QKV.ap[:, s3 * 4:s3 * 4 + 4, 0:C]
                    for j in range(4):
                        cwb = small.ap[:, l, s3 * 16 + j:s3 * 16 + j + 13:4].unsqueeze(2).broadcast_to([128, 4, C])
                        if j == 0:
                            P.op(eng, lambda e, acc4=acc4, cwb=cwb: e.tensor_tensor(out=acc4, in0=PRE.ap[:, :, 0:C], in1=cwb, op=ALU.mult),
                                 reads=[PRE.reg, small], writes=[QKV.reg])
                        else:
                            P.op(eng, lambda e, cwb=cwb, j=j, tmpb=tmpb: e.tensor_tensor(out=tmpb.ap[:, :, 0:C], in0=PRE.ap[:, :, j:j + C],
                                                                                        in1=cwb, op=ALU.mult),
                                 reads=[PRE.reg, small], writes=[tmpb.reg])
                            P.op(eng, lambda e, acc4=acc4, tmpb=tmpb: e.tensor_tensor(out=acc4, in0=acc4, in1=tmpb.ap[:, :, 0:C], op=ALU.add),
                                 reads=[QKV.reg, tmpb.reg], writes=[QKV.reg])
                if si is not None:
                    P.dma("sp", "osc", lambda e: [e.dma_start(out=o_sconv[l, si], in_=HALO.ap)], reads=[HALO.reg])
                P.op("act", lambda e: e.activation(out=QKV.ap[:, :, 0:C], in_=QKV.ap[:, :, 0:C], func=AF.Silu),
                     reads=[QKV.reg], writes=[QKV.reg])
                P.op("act", lambda e: e.activation(out=COLS.ap[:, 39:40], in_=one_c, func=AF.Ln), reads=[COLS.reg], writes=[COLS.reg])
                if BST < 5:
                    return
                RS8 = sb.view(DTb.lo, [8, 128], F32)
                src8 = QKV.ap[:, 0:8, 0:C]
                P.op("dve", lambda e: e.tensor_tensor(out=SQB.ap[:, :, 0:C], in0=src8, in1=src8, op=ALU.mult),
                     reads=[QKV.reg], writes=[SQB.reg])
                for i2 in range(2):
                    pb = bank(4 * C)

                    def mm(e, pb=pb, i2=i2):
                        ins = None
                        for h in range(4):
                            ins = e.matmul(pb.ap[:, h * C:(h + 1) * C], ONES1.ap, SQB.ap[:, i2 * 4 + h, 0:C],
                                           start=True, stop=True)
                        return ins
                    P.op("pe", mm, reads=[ONES1.reg, SQB.reg], writes=[pb])
                    P.op("act", lambda e, pb=pb, i2=i2: e.activation(out=RS8.ap[:, i2 * 4:i2 * 4 + 4, 0:C],
                                                                     in_=pb.ap.rearrange("p (h t) -> p h t", t=C),
                                                                     func=AF.Ln, bias=eps_c.ap),
                         reads=[pb, eps_c], writes=[RS8])
                P.op("act", lambda e: e.activation(out=RS8.ap[:, :, 0:C], in_=RS8.ap[:, :, 0:C], func=AF.Exp, scale=-0.5),
                     reads=[RS8], writes=[RS8])
                P.op("dve", lambda e: e.tensor_tensor(out=src8, in0=src8, in1=RS8.ap[:, :, 0:C], op=ALU.mult),
                     reads=[QKV.reg, RS8], writes=[QKV.reg])
                P.op("act", lambda e: e.activation(out=QN.ap[:, :, 0:C], in_=src8, func=AF.Copy),
                     reads=[QKV.reg], writes=[QN.reg])
                QT = lambda h: QN.ap[:, h, 0:C]
                KT = lambda h: QN.ap[:, 4 + h, 0:C]
                if BST < 9:
                    return
                pkk = bank(4 * C, parts=C)
                pqk = bank(4 * C, parts=C)

                def mm(e):
                    ins = None
                    for h in range(4):
                        e.matmul(pkk.ap[:, h * C:(h + 1) * C], KT(h), KT(h), start=True, stop=True)
                        ins = e.matmul(pqk.ap[:, h * C:(h + 1) * C], KT(h), QT(h), start=True, stop=True)
                    return ins
                P.op("pe", mm, reads=[QN.reg], writes=[pkk, pqk])
                if BST < 10:
                    return
                dt3 = DTb.ap[0:C, :, 0:C]
                for h in range(4):
                    P.op("dve", lambda e, h=h: e.tensor_scalar(out=DTb.ap[0:C, h, 0:C], in0=GCROW.ap[0:C, h, 0:C],
                                                               scalar1=cgc[:, h:h + 1], scalar2=0.0,
                                                               op0=ALU.subtract, op1=ALU.min),
                         reads=[GCROW.reg, COLS.reg], writes=[DTb.reg])
                P.op("act", lambda e: e.activation(out=dt3, in_=dt3, func=AF.Exp), reads=[DTb.reg], writes=[DTb.reg])
                mui = cst.ap[0:C, MU_I, 0:C].unsqueeze(1).broadcast_to([C, 4, C])
                P.op("dve", lambda e: e.tensor_tensor(out=dt3, in0=dt3, in1=mui, op=ALU.mult),
                     reads=[DTb.reg, cst], writes=[DTb.reg])
                P.op("dve", lambda e: e.tensor_tensor(out=QKMT.ap[0:C, :, 0:C], in0=pqk.ap.rearrange("p (h t) -> p h t", t=C),
                                                      in1=dt3, op=ALU.mult), reads=[pqk, DTb.reg], writes=[QKMT.reg])
                if BST < 10.3:
                    return
                b3 = DLb.ap[0:C, :, 0:C]
                for h in range(4):
                    P.op("dve", lambda e, h=h: e.tensor_scalar(out=DLb.ap[0:C, h, 0:C], in0=GCROW.ap[0:C, h, 0:C],
                                                               scalar1=cgc[:, h:h + 1], scalar2=0.0,
                                                               op0=ALU.subtract, op1=ALU.max),
                         reads=[GCROW.reg, COLS.reg], writes=[DLb.reg])
                if BST < 10.4:
                    return
                P.op("act", lambda e: e.activation(out=b3, in_=b3, func=AF.Exp, scale=-1.0), reads=[DLb.reg], writes=[DLb.reg])
                if BST < 10.45:
                    return
                for h in range(4):
                    P.op("dve", lambda e, h=h: e.scalar_tensor_tensor(
                        out=DLb.ap[0:C, h, 0:C], in0=pkk.ap[:, h * C:(h + 1) * C], scalar=cb[:, h:h + 1],
                        in1=DLb.ap[0:C, h, 0:C], op0=ALU.mult, op1=ALU.mult), reads=[pkk, COLS.reg, DLb.reg], writes=[DLb.reg])
                if BST < 10.5:
                    return
                mls = cst.ap[0:C, ML_S, 0:C].unsqueeze(1).broadcast_to([C, 4, C])
                P.op("dve", lambda e: e.tensor_tensor(out=b3, in0=b3, in1=mls, op=ALU.mult),
                     reads=[DLb.reg, cst], writes=[DLb.reg])
                P.op("act", lambda e: e.activation(out=Bb.ap[0:C, :, 0:C], in_=b3, func=AF.Copy), reads=[DLb.reg], writes=[Bb.reg])
                if BST < 10.6:
                    return
                pt = bank(4 * C, parts=C)

                def mm(e):
                    ins = None
                    for h in range(4):
                        ins = e.transpose(pt.ap[:, h * C:(h + 1) * C], DLb.ap[0:C, h, 0:C], idc)
                    return ins
                P.op("pe", mm, reads=[DLb.reg, cst], writes=[pt])
                if BST < 10.7:
                    return
                P.op("act", lambda e: e.activation(out=Ab.ap[0:C, :, 0:C], in_=pt.ap.rearrange("p (h t) -> p h t", t=C),
                                                   func=AF.Copy), reads=[pt], writes=[Ab.reg])
                if BST < 10.8:
                    return
                idb = idc.unsqueeze(1).broadcast_to([C, 4, C])
                P.op("dve", lambda e: e.scalar_tensor_tensor(out=Xb.ap[0:C, :, 0:C], in0=Ab.ap[0:C, :, 0:C], scalar=-1.0, in1=idb,
                                                             op0=ALU.mult, op1=ALU.add), reads=[Ab.reg, cst], writes=[Xb.reg])
                if BST < 11:
                    return
                nlev = {128: 7, 16: 4}[C]
                Ah = lambda h: Ab.ap[0:C, h, 0:C]
                Bh = lambda h: Bb.ap[0:C, h, 0:C]
                Xh = lambda h: Xb.ap[0:C, h, 0:C]
                for k in range(1, nlev):
                    pB = bank(4 * C, parts=C)
                    needA = k < nlev - 1
                    pA = bank(4 * C, parts=C) if needA else None

                    def mm(e, pB=pB):
                        ins = None
                        for h in range(4):
                            ins = e.matmul(pB.ap[:, h * C:(h + 1) * C], Ah(h), Bh(h), start=True, stop=True)
                        return ins
                    P.op("pe", mm, reads=[Ab.reg, Bb.reg], writes=[pB])
                    if needA:
                        def mm(e, pA=pA):
                            ins = None
                            for h in range(4):
                                ins = e.matmul(pA.ap[:, h * C:(h + 1) * C], Bh(h), Ah(h), start=True, stop=True)
                            return ins
                        P.op("pe", mm, reads=[Ab.reg, Bb.reg], writes=[pA])
                    P.op("act", lambda e, pB=pB: e.activation(out=Bb.ap[0:C, :, 0:C],
                                                              in_=pB.ap.rearrange("p (h t) -> p h t", t=C), func=AF.Copy),
                         reads=[pB], writes=[Bb.reg])
                    pX = bank(4 * C, parts=C)

                    def mm(e, pX=pX):
                        ins = None
                        for h in range(4):
                            ins = e.matmul(pX.ap[:, h * C:(h + 1) * C], Bh(h), Xh(h), start=True, stop=True)
                        return ins
                    P.op("pe", mm, reads=[Bb.reg, Xb.reg], writes=[pX])
                    if needA:
                        P.op("dve", lambda e, pA=pA: e.tensor_copy(out=Ab.ap[0:C, :, 0:C],
                                                                   in_=pA.ap.rearrange("p (h t) -> p h t", t=C)),
                             reads=[pA], writes=[Ab.reg])
                    P.op("dve", lambda e, pX=pX: e.tensor_tensor(out=Xb.ap[0:C, :, 0:C], in0=Xb.ap[0:C, :, 0:C],
                                                                 in1=pX.ap.rearrange("p (h t) -> p h t", t=C), op=ALU.add),
                         reads=[pX, Xb.reg], writes=[Xb.reg])
                XF = lambda h: Xb.ap[0:C, h, 0:C]
                if BST < 12:
                    return
                pk = bank(512, parts=C)
                pv = bank(512, parts=C)

                def mm(e):
                    ins = None
                    for h in range(4):
                        e.transpose(pk.ap[:, h * 128:(h + 1) * 128], QKV.ap[:, 4 + h, 0:C], cst.ap[:, IDENT, :])
                        ins = e.transpose(pv.ap[:, h * 128:(h + 1) * 128], QKV.ap[:, 8 + h, 0:C], cst.ap[:, IDENT, :])
                    return ins
                P.op("pe", mm, reads=[QKV.reg, cst], writes=[pk, pv])
                for (dstb, pp, col) in ((KBG, pk, cbg), (KDEC, pk, ckd), (VB, pv, cb)):
                    P.op("dve", lambda e, dstb=dstb, pp=pp, col=col: e.tensor_tensor(
                        out=dstb.ap[0:C, :, :], in0=pp.ap.rearrange("p (h d) -> p h d", d=128),
                        in1=col.unsqueeze(2).broadcast_to([C, 4, 128]), op=ALU.mult),
                        reads=[pp, COLS.reg], writes=[dstb.reg])
                pw = bank(4 * C)

                def mm(e):
                    ins = None
                    for h in range(4):
                        ins = e.matmul(pw.ap[:, h * C:(h + 1) * C], KBG.ap[0:C, h, :], XF(h), start=True, stop=True)
                    return ins
                P.op("pe", mm, reads=[KBG.reg, Xb.reg], writes=[pw])
                P.op("act", lambda e: e.activation(out=WNT.ap[:, :, 0:C], in_=pw.ap.rearrange("p (h t) -> p h t", t=C),
                                                   func=AF.Copy, scale=-1.0), reads=[pw], writes=[WNT.reg])
                P.op("act", lambda e: e.activation(out=DLb.ap[:, :, 0:C], in_=GCROW.ap[:, :, 0:C], func=AF.Exp, bias=lnsc),
                     reads=[GCROW.reg, COLS.reg], writes=[DLb.reg])
                P.op("dve", lambda e: e.tensor_tensor(out=QDT.ap[:, :, 0:C], in0=DLb.ap[:, :, 0:C], in1=QKV.ap[:, 0:4, 0:C],
                                                      op=ALU.mult), reads=[DLb.reg, QKV.reg], writes=[QDT.reg])
                if BST < 14:
                    return
                if next_s1 is not None:
                    nxt = next_s1()
                nb = 2 if aug else 1
                hpb = 4 // nb
                pvn = [bank(hpb * ZW, parts=C) for _ in range(nb)]

                def mm(e):
                    ins = None
                    for h in range(4):
                        o = pvn[h // hpb].ap[:, (h % hpb) * ZW:(h % hpb + 1) * ZW]
                        e.matmul(o[:, SO:SO + 128], XF(h), VB.ap[0:C, h, :], start=True, stop=False)
                        ins = e.matmul(o[:, SO:SO + 128], WNT.ap[:, h, 0:C], ZB.ap[:, h, SO:SO + 128], start=False, stop=True)
                        if aug:
                            ins = e.matmul(o[:, 0:128], WNT.ap[:, h, 0:C], ZB.ap[:, h, 0:128], start=True, stop=True)
                    return ins
                P.op("pe", mm, reads=[Xb.reg, VB.reg, WNT.reg, ZB.reg], writes=pvn)
                for i2 in range(nb):
                    P.op("act" if i2 == 0 else "dve",
                         (lambda e, i2=i2: e.activation(out=VN.ap[0:C, i2 * hpb:(i2 + 1) * hpb, 0:ZW],
                                                        in_=pvn[i2].ap.rearrange("p (h t) -> p h t", t=ZW), func=AF.Copy))
                         if i2 == 0 else
                         (lambda e, i2=i2: e.tensor_copy(out=VN.ap[0:C, i2 * hpb:(i2 + 1) * hpb, 0:ZW],
                                                         in_=pvn[i2].ap.rearrange("p (h t) -> p h t", t=ZW))),
                         reads=[pvn[i2]], writes=[VN.reg])
                po = bank(4 * C)
                pr = bank(4 * C) if aug else None

                def mm(e):
                    ins = None
                    for h in range(4):
                        e.matmul(po.ap[:, h * C:(h + 1) * C], ZB.ap[:, h, SO:SO + 128], QDT.ap[:, h, 0:C], start=True, stop=False)
                        ins = e.matmul(po.ap[:, h * C:(h + 1) * C], VN.ap[0:C, h, SO:SO + 128], QKMT.ap[0:C, h, 0:C],
                                       start=False, stop=True)
                        if aug:
                            e.matmul(pr.ap[:, h * C:(h + 1) * C], ZB.ap[:, h, 0:128], QDT.ap[:, h, 0:C], start=True, stop=False)
                            ins = e.matmul(pr.ap[:, h * C:(h + 1) * C], VN.ap[0:C, h, 0:128], QKMT.ap[0:C, h, 0:C],
                                           start=False, stop=True)
                    return ins
                P.op("pe", mm, reads=[ZB.reg, QDT.reg, VN.reg, QKMT.reg], writes=[po] + ([pr] if aug else []))
                oreg = sb.view(OZT.lo + a * 2, [3 * NT + C], BF16)
                P.op("act", lambda e: e.activation(out=OZT.ap[:, 0:4, a:a + C], in_=po.ap.rearrange("p (h t) -> p h t", t=C),
                                                   func=AF.Copy), reads=[po], writes=[oreg])
                if aug:
                    rreg = sb.view(OZT.lo + (4 * NT + a) * 2, [3 * NT + C], BF16)
                    P.op("dve", lambda e: e.tensor_copy(out=OZT.ap[:, 4:8, a:a + C],
                                                        in_=pr.ap.rearrange("p (h t) -> p h t", t=C)),
                         reads=[pr], writes=[rreg])
                pz = [bank(hpb * ZW) for _ in range(nb)]

                def mm(e):
                    ins = None
                    for h in range(4):
                        o = pz[h // hpb].ap[:, (h % hpb) * ZW:(h % hpb + 1) * ZW]
                        ins = e.matmul(o, KDEC.ap[0:C, h, :], VN.ap[0:C, h, 0:ZW], start=True, stop=True)
                    return ins
                P.op("pe", mm, reads=[KDEC.reg, VN.reg], writes=pz)
                for h in range(4):
                    o = pz[h // hpb].ap[:, (h % hpb) * ZW:(h % hpb + 1) * ZW]
                    P.op("dve", lambda e, h=h, o=o: e.scalar_tensor_tensor(
                        out=Zb.ap[:, h, 0:ZW], in0=Zb.ap[:, h, 0:ZW], scalar=gam[:, h:h + 1], in1=o,
                        op0=ALU.mult, op1=ALU.add), reads=[Zb.reg, COLS.reg, pz[h // hpb]], writes=[Zb.reg])
                P.op("act", lambda e: e.activation(out=ZB.ap[:, :, 0:ZW], in_=Zb.ap[:, :, 0:ZW], func=AF.Copy),
                     reads=[Zb.reg], writes=[ZB.reg])
                return nxt

            P.op("dve", lambda e: e.memset(ONES1.ap, 1.0), writes=[ONES1.reg])
            for si in range(2):
                P.dma("sp", "sld", lambda e, si=si: [e.dma_start(out=Zb.ap[:, :, 0:128],
                                                                  in_=sdelta_d[l, si].rearrange("h p d -> p h d"))],
                      writes=[Zb.reg])
                P.op("act", lambda e: e.activation(out=ZB.ap[:, :, 0:128], in_=Zb.ap[:, :, 0:128], func=AF.Copy),
                     reads=[Zb.reg], writes=[ZB.reg])
                P.dma("sp", "shl", lambda e, si=si: [e.dma_start(out=HALO.ap, in_=sconv_d[:, l, si])], writes=[HALO.reg])
                block(NP + si * 16, 16, 128, 0, si, s1_emit(NP + si * 16, 16))
                P.dma("sp", "osd", lambda e, si=si: [e.dma_start(out=o_sdelta[l, si].rearrange("h p d -> p h d"),
                                                                  in_=Zb.ap[:, :, 0:128])], reads=[Zb.reg])
            P.op("dve", lambda e: e.tensor_copy(out=HALO.ap, in_=HSEL.ap[:, 0:12, HW - 3:HW]), reads=[HSEL.reg], writes=[HALO.reg])
            P.op("dve", lambda e: e.memset(Zb.ap, 0.0), writes=[Zb.reg])
            for h in range(4):
                P.op("dve", lambda e, h=h: e.tensor_copy(out=Zb.ap[:, h, 0:128], in_=cst.ap[:, IDENT, :]),
                     reads=[cst], writes=[Zb.reg])
            P.op("act", lambda e: e.activation(out=ZB.ap, in_=Zb.ap, func=AF.Copy), reads=[Zb.reg], writes=[ZB.reg])
            nblk = cfg.get("nblk", NBLK)
            pre = s1_emit(0, 128)
            for bi in range(nblk):
                nx = (lambda bi=bi: s1_emit((bi + 1) * 128, 128)) if bi + 1 < nblk else None
                pre = block(bi * 128, 128, 256, 128, None, pre, nx)
            ptp = bank(512)

            def mm(e):
                ins = None
                for h in range(4):
                    ins = e.transpose(ptp.ap[:, h * 128:(h + 1) * 128], Zb.ap[:, h, 0:128], cst.ap[:, IDENT, :])
                return ins
            P.op("pe", mm, reads=[Zb.reg, cst], writes=[ptp])
            P.op("act", lambda e: e.activation(out=Zb.ap[:, :, 0:128], in_=ptp.ap.rearrange("p (h t) -> p h t", t=128),
                                               func=AF.Copy), reads=[ptp], writes=[Zb.reg])
            SACC = sb.view(PRE.lo, [4, 128], F32)
            ZR = sb.view(PRE.lo + 2048, [4, 256], F32)
            SNEW = sb.view(PRE.lo + 2048 + 4096, [4, 128], F32)
            SBF = sb.view(PRE.lo + 2048 + 4096 + 2048, [4, 128], BF16)
            assert PRE.lo + 2048 + 4096 + 2048 + 1024 <= KBG.lo
            P.op("dve", lambda e: e.memset(SACC.ap, 0.0), writes=[SACC])
            if XCH:
                rb_in = Reg(f"ccb_in{l}", 0, 1, None)
                rb_out = Reg(f"ccb_out{l}", 0, 1, None)
                P.dma("sp", "xb_st", lambda e: [e.dma_start(out=ccb_in[l][:, :].rearrange("p (h t) -> p h t", t=256), in_=Zb.ap)],
                      reads=[Zb.reg], writes=[rb_in])
                P.dma("pool", "xb_cc", lambda e: [e.collective_compute(
                    "AllGather", ALU.bypass, replica_groups=GROUPS, ins=[ccb_in[l].ap().opt()], outs=[ccb_out[l].ap().opt()])],
                    reads=[rb_in], writes=[rb_out], inc=1)
                for r in range(3):
                    P.dma("sp", "xb_ld", lambda e, r=r: [e.dma_start(
                        out=ZR.ap, in_=ccb_out[l][r * 128:(r + 1) * 128, :].rearrange("p (h t) -> p h t", t=256))],
                        reads=[rb_out], writes=[ZR])
                    pf = bank(512)

                    def mm(e, pf=pf):
                        ins = None
                        for h in range(4):
                            ins = e.matmul(pf.ap[:, h * 128:(h + 1) * 128], ZR.ap[:, h, 0:128], SACC.ap[:, h, :],
                                           start=True, stop=True)
                        return ins
                    P.op("pe", mm, reads=[ZR, SACC], writes=[pf])
                    P.op("dve", lambda e, pf=pf: e.tensor_tensor(out=SNEW.ap, in0=pf.ap.rearrange("p (h t) -> p h t", t=128),
                                                                 in1=ZR.ap[:, :, 128:256], op=ALU.add),
                         reads=[pf, ZR], writes=[SNEW])
                    P.op("dve", lambda e: e.tensor_tensor(out=SNEW.ap, in0=SNEW.ap, in1=SACC.ap, op=ALU.subtract),
                         reads=[SNEW, SACC], writes=[SNEW])
                    P.op("dve", lambda e, r=r: e.scalar_tensor_tensor(out=SACC.ap, in0=SNEW.ap, scalar=perc.ap[:, 4 + r:5 + r],
                                                                      in1=SACC.ap, op0=ALU.mult, op1=ALU.add),
                         reads=[SNEW, SACC, perc], writes=[SACC])
            P.op("act", lambda e: e.activation(out=SBF.ap, in_=SACC.ap, func=AF.Copy), reads=[SACC], writes=[SBF])
            pfin = bank(512)

            def mm(e):
                ins = None
                for h in range(4):
                    ins = e.matmul(pfin.ap[:, h * 128:(h + 1) * 128], Zb.ap[:, h, 0:128], SACC.ap[:, h, :], start=True, stop=True)
                return ins
            P.op("pe", mm, reads=[Zb.reg, SACC], writes=[pfin])
            P.op("dve", lambda e: e.tensor_tensor(out=SNEW.ap, in0=pfin.ap.rearrange("p (h t) -> p h t", t=128),
                                                  in1=Zb.ap[:, :, 128:256], op=ALU.add), reads=[pfin, Zb.reg], writes=[SNEW])
            P.dma("sp", "opd", lambda e: [e.dma_start(out=o_pdelta[l].rearrange("h p d -> p h d"), in_=SNEW.ap)], reads=[SNEW])
            if STG < 3:
                return
            pm = [PRE.lo + 9216]

            def palloc(nbytes):
                lo = pm[0]
                pm[0] += (nbytes + 31) // 32 * 32
                assert pm[0] <= SH1, (pm[0], SH1)
                return lo
            T1 = [palloc(512 * 4) for _ in range(2)]
            T2 = [palloc(512 * 2) for _ in range(2)]
            T3 = [palloc(512 * 2) for _ in range(2)]
            ones128 = sb.view(palloc(128 * 2), [128], BF16)
            P.op("dve", lambda e: e.memset(ones128.ap, 1.0 / 128.0), writes=[ones128])
            ctr = 0
            for h in range(4):
                wgt = wtile(wgate_d[l][h])
                for (a, n) in TT:
                    ctr += 1
                    o = sb.view(OZT.lo + (h * NT + a) * 2, [n], BF16)
                    t1 = sb.view(T1[ctr % 2], [n], F32)
                    t2 = sb.view(T2[ctr % 2], [n], BF16)
                    if a < NP and XCH:
                        pc = bank(n)
                        rt = sb.view(OZT.lo + ((4 + h) * NT + a) * 2, [n], BF16)
                        P.op("pe", lambda e, pc=pc, rt=rt, h=h: e.matmul(pc.ap, SBF.ap[:, h, :], rt.ap, start=True, stop=True),
                             reads=[SBF, rt], writes=[pc])
                        P.op("dve", lambda e, pc=pc, o=o: e.tensor_tensor(out=o.ap, in0=pc.ap, in1=o.ap, op=ALU.add),
                             reads=[pc, o], writes=[o])
                    P.op("dve", lambda e, o=o, t2=t2: e.tensor_tensor(out=t2.ap, in0=o.ap, in1=o.ap, op=ALU.mult),
                         reads=[o], writes=[t2])
                    pss = bank(n)
                    P.op("pe", lambda e, pss=pss, t2=t2: e.matmul(pss.ap, ones128.ap, t2.ap, start=True, stop=True),
                         reads=[ones128, t2], writes=[pss])
                    P.op("act", lambda e, pss=pss, t1=t1: e.activation(out=t1.ap, in_=pss.ap, func=AF.Ln, bias=eps_c.ap),
                         reads=[pss, eps_c], writes=[t1])
                    P.op("act", lambda e, t1=t1: e.activation(out=t1.ap, in_=t1.ap, func=AF.Exp, scale=-0.5),
                         reads=[t1], writes=[t1])
                    P.op("dve", lambda e, o=o, t1=t1: e.scalar_tensor_tensor(out=o.ap, in0=o.ap, scalar=onorm, in1=t1.ap,
                                                                            op0=ALU.mult, op1=ALU.mult),
                         reads=[o, t1, small], writes=[o])
                for (a, n) in TT:
                    ctr += 1
                    o = sb.view(OZT.lo + (h * NT + a) * 2, [n], BF16)
                    t3 = sb.view(T3[ctr % 2], [n], BF16)
                    pg = proj(wgt, xn, a, n)
                    P.op("act", lambda e, pg=pg, t3=t3: e.activation(out=t3.ap, in_=pg.ap, func=AF.Silu),
                         reads=[pg], writes=[t3])
                    P.op("dve", lambda e, o=o, t3=t3: e.tensor_tensor(out=o.ap, in0=o.ap, in1=t3.ap, op=ALU.mult),
                         reads=[o, t3], writes=[o])
            if STG < 4:
                return
            pm[0] = post0
            UP = sb.view(palloc(UPW * 4), [UPW], F32)
            SA = sb.view(palloc(UPW * 4), [UPW], F32)
            SBb = sb.view(palloc(UPW * 4), [UPW], F32)
            DB = sb.view(palloc(NT * 2), [NT], BF16)
            wpl = sb.view(palloc(4 * 128 * 2), [4, 128], BF16)
            t16 = sb.view(palloc(64), [16], F32)
            P.dma("pool", "wpl", lambda e: [e.dma_start(out=wpl.ap, in_=wpool_d[l].rearrange("g c d -> c g d"))],
                  writes=[wpl])
            def ucol(a):
                if a < NP:
                    return HW + a
                s_ = (a - NP) // 16
                return HW + NP + s_ * (HW + 16) + HW + (a - NP - s_ * 16)
            for g in range(4):
                win = 2 << g
                wu_t = wtile(wpu_d[l][g])
                P.op("dve", lambda e, g=g: e.tensor_copy(out=UP.ap[:, 0:HW], in_=HSEL.ap[:, 12 + g, :]), reads=[HSEL.reg], writes=[UP])
                for si in range(2):
                    c0 = HW + NP + si * (HW + 16)
                    P.dma("sp", "sph", lambda e, si=si, c0=c0, g=g: [e.dma_start(out=UP.ap[:, c0:c0 + HW],
                                                                                 in_=spool_d[:, l, si, g, :])], writes=[UP])
                for (a, n) in TT:
                    pb = proj(wu_t, xn, a, n)
                    if a < NP:
                        P.op("act", lambda e, pb=pb, a=a, n=n: e.activation(out=UP.ap[:, HW + a:HW + a + n], in_=pb.ap,
                                                                            func=AF.Copy), reads=[pb], writes=[UP])
                    else:
                        for si in range(2):
                            c0 = ucol(NP + si * 16)
                            P.op("act", lambda e, pb=pb, si=si, c0=c0: e.activation(
                                out=UP.ap[:, c0:c0 + 16], in_=pb.ap[:, si * 16:(si + 1) * 16], func=AF.Copy),
                                reads=[pb], writes=[UP])
                for si in range(2):
                    c0 = ucol(NP + si * 16)
                    P.dma("sp", "osp", lambda e, si=si, c0=c0, g=g: [e.dma_start(out=o_spool[l, si, :, g, :],
                                                                                 in_=UP.ap[:, c0:c0 + 16])], reads=[UP])
                src = UP
                bufs = [SA, SBb]
                step = 1
                k = 0
                while step < win:
                    dst = bufs[k % 2]
                    P.op("dve", lambda e, src=src, dst=dst, step=step: e.tensor_tensor(
                        out=dst.ap[:, step:UPW], in0=src.ap[:, step:UPW], in1=src.ap[:, 0:UPW - step], op=ALU.add),
                        reads=[src], writes=[dst])
                    P.op("dve", lambda e, src=src, dst=dst, step=step: e.tensor_copy(out=dst.ap[:, 0:step], in_=src.ap[:, 0:step]),
                         reads=[src], writes=[dst])
                    src = dst
                    step *= 2
                    k += 1
                for (a, n) in TT:
                    if a < NP:
                        segs = [(a, n, HW + a)]
                    else:
                        segs = [(NP + si * 16, 16, ucol(NP + si * 16)) for si in range(2)]
                    for (ta, tn, uc) in segs:
                        P.op("dve", lambda e, src=src, ta=ta, tn=tn, uc=uc, win=win: e.scalar_tensor_tensor(
                            out=DB.ap[:, ta:ta + tn], in0=src.ap[:, uc:uc + tn], scalar=1.0 / win, in1=UP.ap[:, uc:uc + tn],
                            op0=ALU.mult, op1=ALU.subtract), reads=[src, UP], writes=[DB])
                P.op("dve", lambda e, src=src, g=g: e.tensor_tensor(out=t16.ap, in0=src.ap[:, HW:HW + 16],
                                                                     in1=perc.ap[:, 8 + g * 16:8 + (g + 1) * 16], op=ALU.mult),
                     reads=[src, perc], writes=[t16])
                P.op("dve", lambda e: e.tensor_tensor(out=DB.ap[:, 0:16], in0=t16.ap, in1=UP.ap[:, HW:HW + 16], op=ALU.subtract),
                     reads=[t16, UP], writes=[DB])
                for (a, n) in TT:
                    pb = bank(n)
                    d = sb.view(DB.lo + a * 2, [n], BF16)
                    P.op("pe", lambda e, pb=pb, d=d, g=g: e.matmul(pb.ap, wpl.ap[:, g, :], d.ap, start=True, stop=True),
                         reads=[wpl, d], writes=[pb])
                    z = sb.view(OZT.lo + ((4 + g) * NT + a) * 2, [n], BF16)
                    P.op("dve", lambda e, pb=pb, z=z, g=g: e.tensor_scalar(out=z.ap, in0=pb.ap, scalar1=pscale(g), scalar2=None,
                                                                           op0=ALU.mult), reads=[pb, small], writes=[z])
            if DBG == "oz" and l == 0:
                ozs = sb.view(OZT.lo, [8, NT], BF16)
                P.dma("pool", "dbg", lambda e: [e.dma_start(out=dbg_d[:, 0:256].rearrange("p (c t) -> p c t", t=32),
                                                             in_=OZT.ap[:, :, NP:NT])], reads=[ozs])
            if STG < 5:
                return
            for dc in range(KC):
                wo = wtile(wout_d[l][dc])
                for (a, n) in TT:
                    pb = proj(wo, lambda c, a_, n_: sb.view(OZT.lo + (c * NT + a_) * 2, [n_], BF16), a, n)
                    x = xT(dc, a, n)
                    P.op("dve", lambda e, x=x, pb=pb: e.tensor_tensor(out=x.ap, in0=pb.ap, in1=x.ap, op=ALU.add),
                         reads=[pb, x], writes=[x])

        for l in range(NL):
            if not cfg.get("noffn"):
                ffn(l, 0)
            mixer(l)
            if not cfg.get("noffn"):
                ffn(l, 1)
        rmsnorm(DEPTH * 3)
        for c in range(KC):
            for (a, n) in TT:
                x = xT(c, a, n)
                w = nrmw(DEPTH * 3, c)
                r = rstd(a, n)
                P.op("dve", lambda e, x=x, w=w, r=r: e.scalar_tensor_tensor(
                    out=x.ap, in0=x.ap, scalar=w.ap, in1=r.ap, op0=ALU.mult, op1=ALU.mult),
                    reads=[x, w, r], writes=[x])
        P.dma("sp", "yout", lambda e: [e.dma_start(out=yT_d.rearrange("c p t -> p c t"), in_=xT_all.ap)],
              reads=[xT_all])
        P.wait_all_dma("sp")

        sems = {}
        for i, k in enumerate(P.semkeys):
            sems[k] = es.enter_context(nc.semaphore(f"s{i}"))
        block_ = es.enter_context(nc.Block())

        def run(engobj, name):
            for waits, fn, inc in P.streams[name]:
                for k, v in waits:
                    engobj.wait_ge(sems[k], v)
                if fn is None:
                    continue
                r = fn(engobj)
                if isinstance(r, list):
                    for ins in r:
                        ins.then_inc(sems[inc[0]], inc[1])
                else:
                    r.then_inc(sems[inc[0]], inc[1])

        @block_.tensor
        def _(e):
            run(e, "pe")

        @block_.scalar
        def _(e):
            run(e, "act")

        @block_.vector
        def _(e):
            run(e, "dve")

        @block_.gpsimd
        def _(e):
            run(e, "pool")

        @block_.sync
        def _(e):
            run(e, "sp")
    return nc


def _wtiles(w, col0, nch):
    sub = w[:, col0:col0 + nch * 128]
    return np.ascontiguousarray(sub.reshape(KC, 128, nch, 128).transpose(2, 1, 0, 3))


def _prep_inputs(inp):
    f = np.float32
    xp, xs = inp["x_prompt"], inp["x_sample"]
    nrm = np.zeros((128, DEPTH * 3 + 1, KC), f)
    for l in range(DEPTH):
        for i, nm in enumerate(("norm_ffn1", "norm_mix", "norm_ffn2")):
            nrm[:, l * 3 + i, :] = inp[nm][l].reshape(KC, 128).T
    nrm[:, DEPTH * 3, :] = inp["norm_final"].reshape(KC, 128).T
    shared = {"nrm": nrm}
    small = np.zeros((128, DEPTH, 72), f)
    for l in range(DEPTH):
        for fi, fn in enumerate(("ffn1", "ffn2")):
            shared[f"wg{l}{fi}"] = _wtiles(inp[f"w_{fn}_gate"][l], 0, FC)
            shared[f"wu{l}{fi}"] = _wtiles(inp[f"w_{fn}_up"][l], 0, FC)
            d = inp[f"w_{fn}_down"][l]
            shared[f"wd{l}{fi}"] = np.ascontiguousarray(d.reshape(2, 11, 128, KC, 128).transpose(0, 3, 2, 1, 4))
        wi = inp["w_in"][l]
        shared[f"wqkv{l}"] = _wtiles(wi, 0, 12)
        shared[f"wgate{l}"] = _wtiles(wi, 1536, 4)
        shared[f"wpu{l}"] = _wtiles(wi, 2056, 4)
        shared[f"wab{l}"] = np.ascontiguousarray(wi[:, 2048:2056].reshape(KC, 128, 8).transpose(1, 0, 2))
        shared[f"wout{l}"] = _wtiles(inp["w_out"][l], 0, KC)
        shared[f"wpool{l}"] = np.ascontiguousarray(inp["w_pool"][l])
        cwl = inp["conv_w"][l]
        small[:, l, 0:48] = cwl.reshape(4, 12, 128).transpose(2, 1, 0).reshape(128, 48)
        small[:, l, 48:52] = inp["a_log"][l][None, :]
        small[:, l, 52:56] = inp["dt_bias"][l][None, :]
        small[:, l, 56] = inp["o_norm"][l]
        small[:, l, 57:61] = inp["pool_scale"][l].reshape(4, 128).T
    shared["small"] = small
    cst = np.zeros((128, 5, 128), f)
    ii = np.arange(128)
    cst[:, 0, :] = np.eye(128)
    cst[:, 1, :] = (ii[:, None] <= ii[None, :])
    cst[:, 2, :] = (ii[None, :] > ii[:, None])
    cst[:, 3, :] = (ii[:, None] > ii[None, :])
    cst[:, 4, :] = (ii[None, :] >= ii[:, None]) * np.float32(128.0 ** -0.5)
    shared["consts"] = cst
    maps = []
    for c in range(NCORES):
        b, t = c // 4, c % 4
        tok = np.concatenate([xp[b, t * NP:(t + 1) * NP], xs[2 * c].reshape(16, D), xs[2 * c + 1].reshape(16, D)], 0)
        m = dict(shared)
        m["xT"] = np.ascontiguousarray(tok.T.reshape(KC, 128, NT))
        m["sdelta"] = np.ascontiguousarray(inp["state_delta"][:, 2 * c:2 * c + 2])
        sc = inp["state_conv"][:, 2 * c:2 * c + 2]
        m["sconv"] = np.ascontiguousarray(sc.reshape(DEPTH, 2, 3, 12, 128).transpose(4, 0, 1, 3, 2))
        sp = inp["state_pool"][:, 2 * c:2 * c + 2]
        spl = np.zeros((128, DEPTH, 2, 4, HW), f)
        spl[..., 1:] = sp.reshape(DEPTH, 2, 15, 4, 128).transpose(4, 0, 1, 3, 2)
        m["spool"] = spl
        pc = np.zeros((128, 72), f)
        if t > 0:
            pc[:, t - 1] = 1.0
        for r in range(3):
            pc[:, 4 + r] = 1.0 if r < t else 0.0
        for g in range(4):
            win = 2 << g
            pos = t * NP + np.arange(16)
            pc[:, 8 + g * 16:8 + (g + 1) * 16] = (1.0 / np.minimum(win, pos + 1))[None, :]
        m["percore"] = pc
        maps.append(m)
    return maps


_NC_CACHE = {}


def kernel(**inputs):
    import os
    import json
    inp = {k: np.asarray(v, dtype=np.float32) for k, v in inputs.items()}
    maps = _prep_inputs(inp)
    cfg = json.loads(os.environ.get("KCFG", "{}"))
    if "nc" not in _NC_CACHE:
        _NC_CACHE["nc"] = build_nc(cfg=cfg)
    nc = _NC_CACHE["nc"]
    res = run_bass_kernel_spmd(nc, maps, core_ids=list(range(NCORES)))
    R = res.results
    if cfg.get("dbg"):
        _NC_CACHE["dbg"] = [R[c]["dbg"] for c in range(NCORES)]
    y_p = np.zeros((2, 8192, D), np.float32)
    y_s = np.zeros((16, 16, D), np.float32)
    p_delta = np.zeros((DEPTH, 2, 4, 128, 128), np.float32)
    p_conv = np.zeros((DEPTH, 2, 3, 1536), np.float32)
    p_pool = np.zeros((DEPTH, 2, 15, 512), np.float32)
    s_delta = np.zeros((DEPTH, 16, 4, 128, 128), np.float32)
    s_conv = np.zeros((DEPTH, 16, 3, 1536), np.float32)
    s_pool = np.zeros((DEPTH, 16, 15, 512), np.float32)
    for c in range(NCORES):
        b, t = c // 4, c % 4
        y = R[c]["yT"].reshape(D, NT).T
        y_p[b, t * NP:(t + 1) * NP] = y[:NP]
        y_s[2 * c] = y[NP:NP + 16]
        y_s[2 * c + 1] = y[NP + 16:NP + 32]
        if t == 3:
            p_delta[:, b] = R[c]["o_pdelta"]
            hl = R[c]["o_halo"]
            p_conv[:, b] = hl[:, :, 0:12, HW - 3:].transpose(0, 3, 2, 1).reshape(DEPTH, 3, 1536)
            p_pool[:, b] = hl[:, :, 12:16, 1:].transpose(0, 3, 2, 1).reshape(DEPTH, 15, 512)
        s_delta[:, 2 * c:2 * c + 2] = R[c]["o_sdelta"]
        sc = R[c]["o_sconv"]
        s_conv[:, 2 * c:2 * c + 2] = sc.transpose(0, 1, 4, 3, 2).reshape(DEPTH, 2, 3, 1536)
        spo = R[c]["o_spool"]
        s_pool[:, 2 * c:2 * c + 2] = spo[..., 1:].transpose(0, 1, 4, 3, 2).reshape(DEPTH, 2, 15, 512)
    return (y_p, y_s, p_delta, p_conv, p_pool, s_delta, s_conv, s_pool)
```

```python
import numpy as np
import concourse.bass as bass
import concourse.mybir as mybir
from concourse.bass_utils import run_bass_kernel_spmd
from contextlib import ExitStack

F32 = mybir.dt.float32
F32R = mybir.dt.float32r
BF16 = mybir.dt.bfloat16
AF = mybir.ActivationFunctionType
ALU = mybir.AluOpType

NCORES = 8
D = 1024
KC = 8
DFF = 2816
FC = 22
NP = 2048
NS = 32
NT = NP + NS
DEPTH = 2
EPS = 1e-6
TT = [(0, 512), (512, 512), (1024, 512), (1536, 512), (2048, 32)]
EPOCH = 3000


class Reg:
    __slots__ = ("aid", "lo", "hi", "ap")

    def __init__(self, aid, lo, hi, ap):
        self.aid, self.lo, self.hi, self.ap = aid, lo, hi, ap


class Arena:
    def __init__(self, name, handle_ap, nbytes):
        self.name, self.base, self.nbytes = name, handle_ap, nbytes

    def view(self, lo, shape, dtype, parts=128, p0=0):
        esz = 2 if dtype == BF16 else 4
        n = 1
        for s in shape:
            n *= s
        hi = lo + n * esz
        assert lo % 4 == 0 and hi <= self.nbytes, (self.name, lo, hi, self.nbytes)
        ap = self.base[p0:p0 + parts, lo // 4:(hi + 3) // 4]
        if dtype == BF16:
            ap = ap.bitcast(dtype)[:, 0:n]
        elif dtype == F32R:
            ap = ap.bitcast(dtype)
        if len(shape) == 2:
            ap = ap.rearrange("p (a b) -> p a b", b=shape[1])
        elif len(shape) == 3:
            ap = ap.rearrange("p (a b c) -> p a b c", b=shape[1], c=shape[2])
        return Reg(self.name, lo, hi, ap)


class Prog:
    ENGS = ("pe", "act", "dve", "pool", "sp")

    def __init__(self):
        self.streams = {e: [] for e in self.ENGS}
        self.count = {e: 0 for e in self.ENGS}
        self.records = {}
        self.waited = {e: {} for e in self.ENGS}
        self.semkeys = {}
        self.dma_total = {}

    def _deps(self, eng, reads, writes, is_dma):
        deps = {}

        def add(tok):
            k, v = tok[0], tok[1]
            if deps.get(k, 0) < v:
                deps[k] = v

        for r in reads:
            for rec in self.records.get(r.aid, ()):
                if rec[3] and rec[0] < r.hi and r.lo < rec[1]:
                    add(rec[2])
        for w in writes:
            for rec in self.records.get(w.aid, ()):
                if rec[0] < w.hi and w.lo < rec[1]:
                    if rec[4] == eng and eng == "pe" and not is_dma and not rec[5]:
                        continue
                    add(rec[2])
        return deps

    def _record(self, eng, reads, writes, tok, is_dma):
        for w in writes:
            lst = self.records.setdefault(w.aid, [])
            lst[:] = [r for r in lst if not (w.lo <= r[0] and r[1] <= w.hi)]
            lst.append([w.lo, w.hi, tok, True, eng, is_dma])
        for r in reads:
            lst = self.records.setdefault(r.aid, [])
            if not is_dma:
                lst[:] = [x for x in lst if not (not x[3] and x[4] == eng and not x[5]
                                                 and r.lo <= x[0] and x[1] <= r.hi)]
            lst.append([r.lo, r.hi, tok, False, eng, is_dma])

    def _emit(self, eng, deps, fn, inc):
        waits = []
        wd = self.waited[eng]
        for k, v in deps.items():
            if wd.get(k, 0) < v:
                wd[k] = v
                waits.append((k, v))
        self.streams[eng].append((waits, fn, inc))

    def op(self, eng, fn, reads=(), writes=()):
        deps = self._deps(eng, reads, writes, False)
        self.count[eng] += 1
        n = self.count[eng]
        key = ("e", eng, (n - 1) // EPOCH)
        tok = (key, (n - 1) % EPOCH + 1)
        self.semkeys[key] = True
        self._emit(eng, deps, fn, (key, 1))
        self._record(eng, reads, writes, tok, False)

    def dma(self, eng, slot, fn, reads=(), writes=(), n=1, inc=16):
        deps = self._deps(eng, reads, writes, True)
        key = ("d", slot)
        self.semkeys[key] = True
        tot = self.dma_total.get(key, 0) + inc * n
        self.dma_total[key] = tot
        tok = (key, tot)
        self._emit(eng, deps, fn, (key, inc))
        self._record(eng, reads, writes, tok, True)
        return tok

    def wait_all_dma(self, eng):
        deps = dict(self.dma_total)
        self._emit(eng, deps, None, None)


HW = 16
NBLK = NP // 128


def build_nc(cfg=None):
    cfg = cfg or {}
    NL = cfg.get("nl", DEPTH)
    DBG = cfg.get("dbg", None)
    nc = bass.Bass("TRN2", target_bir_lowering=False)
    P = Prog()

    def din(name, shape):
        return nc.dram_tensor(name, list(shape), F32, kind="ExternalInput").ap()

    def dout(name, shape):
        return nc.dram_tensor(name, list(shape), F32, kind="ExternalOutput").ap()

    xT_d = din("xT", [KC, 128, NT])
    nrm_d = din("nrm", [128, DEPTH * 3 + 1, KC])
    wg_d = [[din(f"wg{l}{f}", [FC, 128, KC, 128]) for f in range(2)] for l in range(DEPTH)]
    wu_d = [[din(f"wu{l}{f}", [FC, 128, KC, 128]) for f in range(2)] for l in range(DEPTH)]
    wd_d = [[din(f"wd{l}{f}", [2, KC, 128, 11, 128]) for f in range(2)] for l in range(DEPTH)]
    wqkv_d = [din(f"wqkv{l}", [12, 128, KC, 128]) for l in range(DEPTH)]
    wgate_d = [din(f"wgate{l}", [4, 128, KC, 128]) for l in range(DEPTH)]
    wpu_d = [din(f"wpu{l}", [4, 128, KC, 128]) for l in range(DEPTH)]
    wab_d = [din(f"wab{l}", [128, KC, 8]) for l in range(DEPTH)]
    wout_d = [din(f"wout{l}", [KC, 128, KC, 128]) for l in range(DEPTH)]
    wpool_d = [din(f"wpool{l}", [4, 128, 128]) for l in range(DEPTH)]
    small_d = din("small", [128, DEPTH, 72])
    consts_d = din("consts", [128, 5, 128])
    sdelta_d = din("sdelta", [DEPTH, 2, 4, 128, 128])
    sconv_d = din("sconv", [128, DEPTH, 2, 12, 3])
    spool_d = din("spool", [128, DEPTH, 2, 4, HW])
    percore_d = din("percore", [128, 8 + 64])
    yT_d = dout("yT", [KC, 128, NT])
    o_pdelta = dout("o_pdelta", [DEPTH, 4, 128, 128])
    o_halo = dout("o_halo", [DEPTH, 128, 16, HW])
    o_sdelta = dout("o_sdelta", [DEPTH, 2, 4, 128, 128])
    o_sconv = dout("o_sconv", [DEPTH, 2, 128, 12, 3])
    o_spool = dout("o_spool", [DEPTH, 2, 128, 4, HW])
    dbg_d = dout("dbg", [128, 4096]) if DBG else None
    XCH = not cfg.get("noxch")
    cca_in = [nc.dram_tensor(f"cca_in{l}", [128, 256], F32) for l in range(DEPTH)]
    cca_out = [nc.dram_tensor(f"cca_out{l}", [4 * 128, 256], F32) for l in range(DEPTH)]
    ccb_in = [nc.dram_tensor(f"ccb_in{l}", [128, 1024], F32) for l in range(DEPTH)]
    ccb_out = [nc.dram_tensor(f"ccb_out{l}", [4 * 128, 1024], F32) for l in range(DEPTH)]
    GROUPS = [[0, 1, 2, 3], [4, 5, 6, 7]]

    with ExitStack() as es:
        SB_BYTES = cfg.get("sb", 201) * 1024 + 512
        SBR_BYTES = 3 * 4 * 128 * 4
        sbr_t = es.enter_context(nc.sbuf_tensor("arena_r", [128, SBR_BYTES // 4], F32))
        sb_t = es.enter_context(nc.sbuf_tensor("arena", [128, SB_BYTES // 4], F32))
        ps_t = es.enter_context(nc.psum_tensor("psum", [128, 8 * 512], F32))
        sb = Arena("sb", sb_t[:, :], SB_BYTES)
        sbr = Arena("sbr", sbr_t[:, :], SBR_BYTES)
        roff = [0]
        ps = Arena("ps", ps_t[:, :], 8 * 2048)

        off = [0]

        def alloc(nbytes):
            lo = off[0]
            off[0] += (nbytes + 31) // 32 * 32
            assert off[0] <= SB_BYTES, (off[0], SB_BYTES)
            return lo

        XT = alloc(KC * NT * 4)
        XN = alloc(KC * NT * 2)
        NRM = alloc((DEPTH * 3 + 1) * KC * 4)
        SMALL = alloc(DEPTH * 72 * 4)
        PERC = alloc(72 * 4)
        CST = alloc(5 * 128 * 4)
        CSTR = alloc(2 * 128 * 4)
        ONES = alloc(128 * 2)
        EPSC = alloc(32)
        WAB = alloc(KC * 8 * 2)
        WX = alloc(KC * 128 * 2)
        NW = 6
        WG = [alloc(KC * 128 * 2) for _ in range(NW)]
        WU = [alloc(KC * 128 * 2) for _ in range(NW)]
        SH0 = off[0]
        HID = alloc(11 * NT * 2)
        RSTD = alloc(NT * 4)
        NWD = 3
        WD = [alloc(11 * 128 * 2) for _ in range(NWD)]
        NSG = 3
        SG = [alloc(512 * 2) for _ in range(NSG)]
        SH1 = SB_BYTES

        def xT(c, a, n):
            return sb.view(XT + (c * NT + a) * 4, [n], F32)

        def xn(c, a, n):
            return sb.view(XN + (c * NT + a) * 2, [n], BF16)

        def hid(j, a, n):
            return sb.view(HID + (j * NT + a) * 2, [n], BF16)

        def rstd(a, n):
            return sb.view(RSTD + a * 4, [n], F32)

        def nrmw(i, c):
            return sb.view(NRM + (i * KC + c) * 4, [1], F32)

        ones_bf = sb.view(ONES, [128], BF16)
        eps_c = sb.view(EPSC, [1], F32)
        nrm_all = sb.view(NRM, [(DEPTH * 3 + 1) * KC], F32)
        xT_all = sb.view(XT, [KC, NT], F32)
        small = sb.view(SMALL, [DEPTH, 72], F32)
        perc = sb.view(PERC, [72], F32)
        cst = sb.view(CST, [5, 128], F32)
        ones_r = sb.view(CSTR, [128], F32R)
        IDENT, TRIU, MU_S, ML_S, MU_I = range(5)

        bank_ctr = [0]

        def bank(n=512, parts=128):
            b = bank_ctr[0] % 8
            bank_ctr[0] += 1
            return ps.view(b * 2048, [n], F32, parts=parts)

        dbg_off = [0]

        def dbg(name, reg, ap, parts, n):
            if DBG != name:
                return
            o = dbg_off[0]
            dbg_off[0] += n
            P.dma("sp", "dbg", lambda e: [e.dma_start(out=dbg_d[0:parts, o:o + n], in_=ap)], reads=[reg])

        P.dma("sp", "xin", lambda e: [e.dma_start(out=xT_all.ap, in_=xT_d.rearrange("c p t -> p c t"))],
              writes=[xT_all])
        P.dma("sp", "c0", lambda e: [e.dma_start(out=nrm_all.ap, in_=nrm_d.rearrange("p i c -> p (i c)"))],
              writes=[nrm_all])
        P.dma("sp", "c1", lambda e: [e.dma_start(out=small.ap, in_=small_d)], writes=[small])
        P.dma("sp", "c2", lambda e: [e.dma_start(out=perc.ap, in_=percore_d)], writes=[perc])
        P.dma("sp", "c3", lambda e: [e.dma_start(out=cst.ap, in_=consts_d)], writes=[cst])
        P.op("dve", lambda e: e.memset(ones_bf.ap, 1.0 / D), writes=[ones_bf])
        P.op("dve", lambda e: e.memset(eps_c.ap, EPS), writes=[eps_c])

        def rmsnorm(ni, order=None):
            for (a, n) in (order or TT):
                pb = bank(n)
                sqs = []
                for c in range(KC):
                    s = hid(c, a, n)
                    x = xT(c, a, n)
                    P.op("act", lambda e, s=s, x=x: e.activation(out=s.ap, in_=x.ap, func=AF.Square),
                         reads=[x], writes=[s])
                    sqs.append(s)

                def mm(e, pb=pb, sqs=sqs):
                    ins = None
                    for c in range(KC):
                        ins = e.matmul(pb.ap, ones_bf.ap, sqs[c].ap, start=(c == 0), stop=(c == KC - 1))
                    return ins
                P.op("pe", mm, reads=[ones_bf] + sqs, writes=[pb])
                r = rstd(a, n)
                P.op("act", lambda e, r=r, pb=pb: e.activation(out=r.ap, in_=pb.ap, func=AF.Sqrt, bias=eps_c.ap),
                     reads=[pb, eps_c], writes=[r])
                P.op("dve", lambda e, r=r: e.reciprocal(out=r.ap, in_=r.ap), reads=[r], writes=[r])
                for c in range(KC):
                    x = xT(c, a, n)
                    o = xn(c, a, n)
                    w = nrmw(ni, c)
                    P.op("dve", lambda e, o=o, x=x, w=w, r=r: e.scalar_tensor_tensor(
                        out=o.ap, in0=x.ap, scalar=w.ap, in1=r.ap, op0=ALU.mult, op1=ALU.mult),
                        reads=[x, w, r], writes=[o])

        wctr = [0]
        wdctr = [0]
        sgctr = [0]

        def wtile(src_ap):
            slot = wctr[0] % (2 * NW)
            wctr[0] += 1
            w = sb.view((WG + WU)[slot], [KC, 128], BF16)
            P.dma("pool", f"w{slot}", lambda e: [e.dma_start(out=w.ap, in_=src_ap)], writes=[w])
            return w

        def proj(w, xs_fn, a, n, pb=None):
            pb = pb or bank(n)
            xs = [xs_fn(c, a, n) for c in range(KC)]

            def mm(e):
                ins = None
                for c in range(KC):
                    ins = e.matmul(pb.ap, w.ap[:, c, :], xs[c].ap, start=(c == 0), stop=(c == KC - 1))
                return ins
            P.op("pe", mm, reads=[w] + xs, writes=[pb])
            return pb

        def ffn(l, f):
            rmsnorm(l * 3 + (0 if f == 0 else 2))
            for half in range(2):
                for jj in range(11):
                    j = half * 11 + jj
                    wg = wtile(wg_d[l][f][j])
                    wu = wtile(wu_d[l][f][j])
                    for (a, n) in TT:
                        pg = proj(wg, xn, a, n)
                        pu = proj(wu, xn, a, n)
                        sg = sb.view(SG[sgctr[0] % NSG], [n], BF16)
                        sgctr[0] += 1
                        P.op("act", lambda e, sg=sg, pg=pg: e.activation(out=sg.ap, in_=pg.ap, func=AF.Silu),
                             reads=[pg], writes=[sg])
                        h = hid(jj, a, n)
                        P.op("dve", lambda e, h=h, sg=sg, pu=pu: e.tensor_tensor(out=h.ap, in0=sg.ap, in1=pu.ap,
                                                                                op=ALU.mult),
                             reads=[sg, pu], writes=[h])
                for dc in range(KC):
                    slot = wdctr[0] % NWD
                    wdctr[0] += 1
                    wd = sb.view(WD[slot], [11, 128], BF16)
                    P.dma("pool", f"wd{slot}", lambda e, wd=wd, dc=dc, half=half: [
                        e.dma_start(out=wd.ap, in_=wd_d[l][f][half, dc])], writes=[wd])
                    for (a, n) in TT:
                        pb = bank(n)
                        hs = [hid(jj, a, n) for jj in range(11)]

                        def mm(e, pb=pb, wd=wd, hs=hs):
                            ins = None
                            for jj in range(11):
                                ins = e.matmul(pb.ap, wd.ap[:, jj, :], hs[jj].ap, start=(jj == 0), stop=(jj == 10))
                            return ins
                        P.op("pe", mm, reads=[wd] + hs, writes=[pb])
                        x = xT(dc, a, n)
                        P.op("dve", lambda e, x=x, pb=pb: e.scalar_tensor_tensor(
                            out=x.ap, in0=pb.ap, scalar=0.5, in1=x.ap, op0=ALU.mult, op1=ALU.add),
                            reads=[pb, x], writes=[x])

        moff = [SH0]

        def malloc(nbytes):
            lo = moff[0]
            moff[0] += (nbytes + 31) // 32 * 32
            assert moff[0] <= SH1, (moff[0], SH1)
            return lo

        class MB:
            def __init__(self, shape, parts=128, dtype=F32, rr=False):
                n = 1
                for s_ in shape:
                    n *= s_
                self.shape, self.dtype = shape, dtype
                if rr:
                    self.lo = roff[0]
                    roff[0] += n * 4
                    self.reg = sbr.view(self.lo, shape, F32)
                    self.r = sbr.view(self.lo, shape, F32R).ap
                else:
                    self.lo = malloc(n * (2 if dtype == BF16 else 4))
                    self.reg = sb.view(self.lo, shape, dtype)
                    self.r = None
                self.ap = self.reg.ap

        OZT = MB([8, NT], dtype=BF16)
        COLS = MB([40])
        H16 = MB([16, HW])
        HSEL = MB([16, HW])
        HTMP = MB([16, HW])
        PRE = MB([4, 3 + 128])
        HALO = MB([12, 3])
        QKV = MB([12, 128])
        GCROW = MB([4, 128])
        DTb = MB([4, 128])
        DLb = MB([4, 128])
        Ab = MB([4, 128], dtype=BF16)
        Bb = MB([4, 128], dtype=BF16)
        Xb = MB([4, 128], dtype=BF16)
        QN = MB([8, 128], dtype=BF16)
        SQB = MB([8, 128], dtype=BF16)
        KBG = MB([4, 128], dtype=BF16)
        KDEC = MB([4, 128], dtype=BF16)
        VB = MB([4, 128], dtype=BF16)
        WNT = MB([4, 128], dtype=BF16)
        QDT = MB([4, 128], dtype=BF16)
        QKMT = MB([4, 128], dtype=BF16)
        VN = MB([4, 256], dtype=BF16)
        Zb = MB([4, 256])
        ZB = MB([4, 256], dtype=BF16)
        ONES1 = MB([128], dtype=BF16)
        post0 = PRE.lo
        UPW = HW + NP + 2 * (HW + 16)

        def mixer(l):
            cw = lambda ch, j: small.ap[:, l, ch * 4 + j: ch * 4 + j + 1]
            alog = small.ap[:, l, 48:52]
            dtb = small.ap[:, l, 52:56]
            onorm = small.ap[:, l, 56:57]
            pscale = lambda g: small.ap[:, l, 57 + g:58 + g]
            rmsnorm(l * 3 + 1, order=[TT[3], TT[0], TT[1], TT[2], TT[4]])
            wq = []
            for ch in range(12):
                w = sb.view((WG + WU)[ch], [KC, 128], BF16)
                P.dma("pool", f"w{ch}", lambda e, w=w, ch=ch: [e.dma_start(out=w.ap, in_=wqkv_d[l][ch])], writes=[w])
                wq.append(w)
            wctr[0] = 0
            wab = sb.view(WAB, [KC, 8], BF16)
            P.dma("pool", "wab", lambda e: [e.dma_start(out=wab.ap, in_=wab_d[l])], writes=[wab])
            negA = COLS.ap[:, 32:36]
            P.op("act", lambda e: e.activation(out=negA, in_=alog, func=AF.Exp), reads=[small], writes=[COLS.reg])
            P.op("dve", lambda e: e.tensor_scalar(out=negA, in0=negA, scalar1=-1.0, scalar2=None, op0=ALU.mult),
                 reads=[COLS.reg], writes=[COLS.reg])
            one_c = COLS.ap[:, 36:37]
            P.op("dve", lambda e: e.memset(one_c, 1.0), writes=[COLS.reg])
            lnsc = COLS.ap[:, 37:38]
            P.op("dve", lambda e: e.memset(lnsc, -0.5 * float(np.log(128.0))), writes=[COLS.reg])

            hb = bank(16 * HW)
            wx = sb.view(WX, [KC, 128], BF16)
            for ch in range(16):
                if ch < 12:
                    wt = wq[ch]
                else:
                    wt = wx
                    P.dma("pool", "wx", lambda e, ch=ch: [e.dma_start(out=wx.ap, in_=wpu_d[l][ch - 12])], writes=[wx])
                xs = [xn(c, NP - HW, HW) for c in range(KC)]

                def mm(e, ch=ch, xs=xs, wt=wt):
                    ins = None
                    for c in range(KC):
                        ins = e.matmul(hb.ap[:, ch * HW:(ch + 1) * HW], wt.ap[:, c, :], xs[c].ap,
                                       start=(c == 0), stop=(c == KC - 1))
                    return ins
                P.op("pe", mm, reads=[wt] + xs, writes=[hb])
            P.op("act", lambda e: e.activation(out=H16.ap, in_=hb.ap.rearrange("p (c t) -> p c t", t=HW),
                                               func=AF.Copy), reads=[hb], writes=[H16.reg])
            P.dma("sp", "ohl", lambda e: [e.dma_start(out=o_halo[l], in_=H16.ap)], reads=[H16.reg])
            if XCH:
                ra_in = Reg(f"cca_in{l}", 0, 1, None)
                ra_out = Reg(f"cca_out{l}", 0, 1, None)
                P.dma("sp", "xa_st", lambda e: [e.dma_start(out=cca_in[l][:, :].rearrange("p (c t) -> p c t", t=HW), in_=H16.ap)],
                      reads=[H16.reg], writes=[ra_in])
                P.dma("pool", "xa_cc", lambda e: [e.collective_compute(
                    "AllGather", ALU.bypass, replica_groups=GROUPS, ins=[cca_in[l].ap().opt()], outs=[cca_out[l].ap().opt()])],
                    reads=[ra_in], writes=[ra_out], inc=1)
                for r in range(4):
                    P.dma("sp", "xa_ld", lambda e, r=r: [e.dma_start(
                        out=HTMP.ap, in_=cca_out[l][r * 128:(r + 1) * 128, :].rearrange("p (c t) -> p c t", t=HW))],
                        reads=[ra_out], writes=[HTMP.reg])
                    if r == 0:
                        P.op("dve", lambda e: e.tensor_scalar(out=HSEL.ap, in0=HTMP.ap, scalar1=perc.ap[:, 0:1], scalar2=None,
                                                              op0=ALU.mult), reads=[HTMP.reg, perc], writes=[HSEL.reg])
                    else:
                        P.op("dve", lambda e, r=r: e.scalar_tensor_tensor(out=HSEL.ap, in0=HTMP.ap, scalar=perc.ap[:, r:r + 1],
                                                                          in1=HSEL.ap, op0=ALU.mult, op1=ALU.add),
                             reads=[HTMP.reg, perc, HSEL.reg], writes=[HSEL.reg])
            else:
                P.op("dve", lambda e: e.memset(HSEL.ap, 0.0), writes=[HSEL.reg])
            STG = cfg.get('stage', 99)
            BST = cfg.get('bstage', 99)
            if STG < 1:
                return

            def block(a, C, ZW, SO, si):
                aug = ZW == 256
                idc = cst.ap[0:C, IDENT, 0:C]
                pbs = []
                for s3 in range(3):
                    pb = bank(4 * C)
                    xs = [xn(c, a, C) for c in range(KC)]

                    def mm(e, pb=pb, s3=s3, xs=xs):
                        ins = None
                        for h in range(4):
                            for c in range(KC):
                                ins = e.matmul(pb.ap[:, h * C:(h + 1) * C], wq[s3 * 4 + h].ap[:, c, :], xs[c].ap,
                                               start=(c == 0), stop=(c == KC - 1))
                        return ins
                    P.op("pe", mm, reads=wq[s3 * 4:s3 * 4 + 4] + xs, writes=[pb])
                    pbs.append(pb)
                pab = bank(8, parts=C)
                xs = [xn(c, a, C) for c in range(KC)]

                def mm(e, xs=xs):
                    ins = None
                    for c in range(KC):
                        ins = e.matmul(pab.ap, xs[c].ap, wab.ap[:, c, :], start=(c == 0), stop=(c == KC - 1))
                    return ins
                P.op("pe", mm, reads=[wab] + xs, writes=[pab])
                if BST < 6:
                    return
                cg = COLS.ap[0:C, 0:4]
                cb = COLS.ap[0:C, 4:8]
                cgc = COLS.ap[0:C, 8:12]
                cbg = COLS.ap[0:C, 12:16]
                ckd = COLS.ap[0:C, 16:20]
                ct = COLS.ap[0:C, 20:24]
                gam = COLS.ap[:, 24:28]
                P.op("dve", lambda e: e.tensor_tensor(out=ct, in0=pab.ap[:, 0:4], in1=dtb[0:C, :], op=ALU.add),
                     reads=[pab, small], writes=[COLS.reg])
                P.op("act", lambda e: e.activation(out=ct, in_=ct, func=AF.Exp), reads=[COLS.reg], writes=[COLS.reg])
                P.op("act", lambda e: e.activation(out=ct, in_=ct, func=AF.Ln, bias=one_c[0:C, :]),
                     reads=[COLS.reg], writes=[COLS.reg])
                P.op("dve", lambda e: e.tensor_tensor(out=cg, in0=ct, in1=negA[0:C, :], op=ALU.mult),
                     reads=[COLS.reg], writes=[COLS.reg])
                P.op("act", lambda e: e.activation(out=cb, in_=pab.ap[:, 4:8], func=AF.Exp, scale=-1.0),
                     reads=[pab], writes=[COLS.reg])
                P.op("dve", lambda e: e.tensor_scalar(out=cb, in0=cb, scalar1=1.0, scalar2=None, op0=ALU.add),
                     reads=[COLS.reg], writes=[COLS.reg])
                P.op("dve", lambda e: e.reciprocal(out=cb, in_=cb), reads=[COLS.reg], writes=[COLS.reg])
                if BST < 7:
                    return
                pgc = bank(4, parts=C)
                P.op("pe", lambda e: e.matmul(pgc.ap, cst.ap[0:C, TRIU, 0:C], cg, start=True, stop=True),
                     reads=[cst, COLS.reg], writes=[pgc])
                P.op("dve", lambda e: e.tensor_copy(out=DLb.ap[0:C, :, :],
                                                    in_=cg.unsqueeze(2).broadcast_to([C, 4, 128])),
                     reads=[COLS.reg], writes=[DLb.reg])
                pgr = bank(4 * C)

                def mm(e):
                    ins = None
                    for h in range(4):
                        ins = e.matmul(pgr.ap[:, h * C:(h + 1) * C], DLb.ap[0:C, h, :], cst.ap[0:C, TRIU, 0:C],
                                       start=True, stop=True)
                    return ins
                P.op("pe", mm, reads=[cst, DLb.reg], writes=[pgr])
                P.op("dve", lambda e: e.tensor_copy(out=cgc, in_=pgc.ap), reads=[pgc], writes=[COLS.reg])
                P.op("act", lambda e: e.activation(out=GCROW.ap[:, :, 0:C], in_=pgr.ap.rearrange("p (h t) -> p h t", t=C),
                                                   func=AF.Copy), reads=[pgr], writes=[GCROW.reg])
                P.op("act", lambda e: e.activation(out=cbg, in_=cgc, func=AF.Exp), reads=[COLS.reg], writes=[COLS.reg])
                P.op("dve", lambda e: e.tensor_tensor(out=cbg, in0=cbg, in1=cb, op=ALU.mult),
                     reads=[COLS.reg], writes=[COLS.reg])
                P.op("dve", lambda e: e.tensor_tensor(out=ckd, in0=GCROW.ap[0:C, :, C - 1], in1=cgc, op=ALU.subtract),
                     reads=[GCROW.reg, COLS.reg], writes=[COLS.reg])
                P.op("act", lambda e: e.activation(out=ckd, in_=ckd, func=AF.Exp), reads=[COLS.reg], writes=[COLS.reg])
                P.op("act", lambda e: e.activation(out=gam, in_=GCROW.ap[:, :, C - 1], func=AF.Exp),
                     reads=[GCROW.reg], writes=[COLS.reg])
                P.op("act", lambda e: e.activation(out=COLS.ap[:, 38:39], in_=one_c, func=AF.Silu), reads=[COLS.reg], writes=[COLS.reg])
                if BST < 3:
                    return
                for s3 in range(3):
                    pb = pbs[s3]
                    P.op("act", lambda e, pb=pb: e.activation(out=PRE.ap[:, :, 3:3 + C],
                                                              in_=pb.ap.rearrange("p (h t) -> p h t", t=C),
                                                              func=AF.Copy), reads=[pb], writes=[PRE.reg])
                    P.op("dve", lambda e, s3=s3: e.tensor_copy(out=PRE.ap[:, :, 0:3], in_=HALO.ap[:, s3 * 4:s3 * 4 + 4, :]),
                         reads=[HALO.reg], writes=[PRE.reg])
                    P.op("dve", lambda e, s3=s3: e.tensor_copy(out=HALO.ap[:, s3 * 4:s3 * 4 + 4, :], in_=PRE.ap[:, :, C:C + 3]),
                         reads=[PRE.reg], writes=[HALO.reg])
                    eng = "dve"
                    tmpb = DTb
                    acc4 = QKV.ap[:, s3 * 4:s3 * 4 + 4, 0:C]
                    for j in range(4):
                        cwb = small.ap[:, l, s3 * 16 + j:s3 * 16 + j + 13:4].unsqueeze(2).broadcast_to([128, 4, C])
                        if j == 0:
                            P.op(eng, lambda e, acc4=acc4, cwb=cwb: e.tensor_tensor(out=acc4, in0=PRE.ap[:, :, 0:C], in1=cwb, op=ALU.mult),
                                 reads=[PRE.reg, small], writes=[QKV.reg])
                        else:
                            P.op(eng, lambda e, cwb=cwb, j=j, tmpb=tmpb: e.tensor_tensor(out=tmpb.ap[:, :, 0:C], in0=PRE.ap[:, :, j:j + C],
                                                                                        in1=cwb, op=ALU.mult),
                                 reads=[PRE.reg, small], writes=[tmpb.reg])
                            P.op(eng, lambda e, acc4=acc4, tmpb=tmpb: e.tensor_tensor(out=acc4, in0=acc4, in1=tmpb.ap[:, :, 0:C], op=ALU.add),
                                 reads=[QKV.reg, tmpb.reg], writes=[QKV.reg])
                if si is not None:
                    P.dma("sp", "osc", lambda e: [e.dma_start(out=o_sconv[l, si], in_=HALO.ap)], reads=[HALO.reg])
                P.op("act", lambda e: e.activation(out=QKV.ap[:, :, 0:C], in_=QKV.ap[:, :, 0:C], func=AF.Silu),
                     reads=[QKV.reg], writes=[QKV.reg])
                P.op("act", lambda e: e.activation(out=COLS.ap[:, 39:40], in_=one_c, func=AF.Ln), reads=[COLS.reg], writes=[COLS.reg])
                if BST < 5:
                    return
                RS8 = sb.view(DTb.lo, [8, 128], F32)
                src8 = QKV.ap[:, 0:8, 0:C]
                P.op("dve", lambda e: e.tensor_tensor(out=SQB.ap[:, :, 0:C], in0=src8, in1=src8, op=ALU.mult),
                     reads=[QKV.reg], writes=[SQB.reg])
                for i2 in range(2):
                    pb = bank(4 * C)

                    def mm(e, pb=pb, i2=i2):
                        ins = None
                        for h in range(4):
                            ins = e.matmul(pb.ap[:, h * C:(h + 1) * C], ONES1.ap, SQB.ap[:, i2 * 4 + h, 0:C],
                                           start=True, stop=True)
                        return ins
                    P.op("pe", mm, reads=[ONES1.reg, SQB.reg], writes=[pb])
                    P.op("act", lambda e, pb=pb, i2=i2: e.activation(out=RS8.ap[:, i2 * 4:i2 * 4 + 4, 0:C],
                                                                     in_=pb.ap.rearrange("p (h t) -> p h t", t=C),
                                                                     func=AF.Ln, bias=eps_c.ap),
                         reads=[pb, eps_c], writes=[RS8])
                P.op("act", lambda e: e.activation(out=RS8.ap[:, :, 0:C], in_=RS8.ap[:, :, 0:C], func=AF.Exp, scale=-0.5),
                     reads=[RS8], writes=[RS8])
                P.op("dve", lambda e: e.tensor_tensor(out=src8, in0=src8, in1=RS8.ap[:, :, 0:C], op=ALU.mult),
                     reads=[QKV.reg, RS8], writes=[QKV.reg])
                P.op("act", lambda e: e.activation(out=QN.ap[:, :, 0:C], in_=src8, func=AF.Copy),
                     reads=[QKV.reg], writes=[QN.reg])
                QT = lambda h: QN.ap[:, h, 0:C]
                KT = lambda h: QN.ap[:, 4 + h, 0:C]
                if BST < 9:
                    return
                pkk = bank(4 * C, parts=C)
                pqk = bank(4 * C, parts=C)

                def mm(e):
                    ins = None
                    for h in range(4):
                        e.matmul(pkk.ap[:, h * C:(h + 1) * C], KT(h), KT(h), start=True, stop=True)
                        ins = e.matmul(pqk.ap[:, h * C:(h + 1) * C], KT(h), QT(h), start=True, stop=True)
                    return ins
                P.op("pe", mm, reads=[QN.reg], writes=[pkk, pqk])
                if BST < 10:
                    return
                dt3 = DTb.ap[0:C, :, 0:C]
                for h in range(4):
                    P.op("dve", lambda e, h=h: e.tensor_scalar(out=DTb.ap[0:C, h, 0:C], in0=GCROW.ap[0:C, h, 0:C],
                                                               scalar1=cgc[:, h:h + 1], scalar2=0.0,
                                                               op0=ALU.subtract, op1=ALU.min),
                         reads=[GCROW.reg, COLS.reg], writes=[DTb.reg])
                P.op("act", lambda e: e.activation(out=dt3, in_=dt3, func=AF.Exp), reads=[DTb.reg], writes=[DTb.reg])
                mui = cst.ap[0:C, MU_I, 0:C].unsqueeze(1).broadcast_to([C, 4, C])
                P.op("dve", lambda e: e.tensor_tensor(out=dt3, in0=dt3, in1=mui, op=ALU.mult),
                     reads=[DTb.reg, cst], writes=[DTb.reg])
                P.op("dve", lambda e: e.tensor_tensor(out=QKMT.ap[0:C, :, 0:C], in0=pqk.ap.rearrange("p (h t) -> p h t", t=C),
                                                      in1=dt3, op=ALU.mult), reads=[pqk, DTb.reg], writes=[QKMT.reg])
                if BST < 10.3:
                    return
                b3 = DLb.ap[0:C, :, 0:C]
                for h in range(4):
                    P.op("dve", lambda e, h=h: e.tensor_scalar(out=DLb.ap[0:C, h, 0:C], in0=GCROW.ap[0:C, h, 0:C],
                                                               scalar1=cgc[:, h:h + 1], scalar2=0.0,
                                                               op0=ALU.subtract, op1=ALU.max),
                         reads=[GCROW.reg, COLS.reg], writes=[DLb.reg])
                if BST < 10.4:
                    return
                P.op("act", lambda e: e.activation(out=b3, in_=b3, func=AF.Exp, scale=-1.0), reads=[DLb.reg], writes=[DLb.reg])
                if BST < 10.45:
                    return
                for h in range(4):
                    P.op("dve", lambda e, h=h: e.scalar_tensor_tensor(
                        out=DLb.ap[0:C, h, 0:C], in0=pkk.ap[:, h * C:(h + 1) * C], scalar=cb[:, h:h + 1],
                        in1=DLb.ap[0:C, h, 0:C], op0=ALU.mult, op1=ALU.mult), reads=[pkk, COLS.reg, DLb.reg], writes=[DLb.reg])
                if BST < 10.5:
                    return
                mls = cst.ap[0:C, ML_S, 0:C].unsqueeze(1).broadcast_to([C, 4, C])
                P.op("dve", lambda e: e.tensor_tensor(out=b3, in0=b3, in1=mls, op=ALU.mult),
                     reads=[DLb.reg, cst], writes=[DLb.reg])
                P.op("act", lambda e: e.activation(out=Bb.ap[0:C, :, 0:C], in_=b3, func=AF.Copy), reads=[DLb.reg], writes=[Bb.reg])
                if BST < 10.6:
                    return
                pt = bank(4 * C, parts=C)

                def mm(e):
                    ins = None
                    for h in range(4):
                        ins = e.transpose(pt.ap[:, h * C:(h + 1) * C], DLb.ap[0:C, h, 0:C], idc)
                    return ins
                P.op("pe", mm, reads=[DLb.reg, cst], writes=[pt])
                if BST < 10.7:
                    return
                P.op("act", lambda e: e.activation(out=Ab.ap[0:C, :, 0:C], in_=pt.ap.rearrange("p (h t) -> p h t", t=C),
                                                   func=AF.Copy), reads=[pt], writes=[Ab.reg])
                if BST < 10.8:
                    return
                idb = idc.unsqueeze(1).broadcast_to([C, 4, C])
                P.op("dve", lambda e: e.scalar_tensor_tensor(out=Xb.ap[0:C, :, 0:C], in0=Ab.ap[0:C, :, 0:C], scalar=-1.0, in1=idb,
                                                             op0=ALU.mult, op1=ALU.add), reads=[Ab.reg, cst], writes=[Xb.reg])
                if BST < 11:
                    return
                nlev = {128: 7, 16: 4}[C]
                Ah = lambda h: Ab.ap[0:C, h, 0:C]
                Bh = lambda h: Bb.ap[0:C, h, 0:C]
                Xh = lambda h: Xb.ap[0:C, h, 0:C]
                for k in range(1, nlev):
                    pB = bank(4 * C, parts=C)
                    needA = k < nlev - 1
                    pA = bank(4 * C, parts=C) if needA else None

                    def mm(e, pB=pB):
                        ins = None
                        for h in range(4):
                            ins = e.matmul(pB.ap[:, h * C:(h + 1) * C], Ah(h), Bh(h), start=True, stop=True)
                        return ins
                    P.op("pe", mm, reads=[Ab.reg, Bb.reg], writes=[pB])
                    if needA:
                        def mm(e, pA=pA):
                            ins = None
                            for h in range(4):
                                ins = e.matmul(pA.ap[:, h * C:(h + 1) * C], Bh(h), Ah(h), start=True, stop=True)
                            return ins
                        P.op("pe", mm, reads=[Ab.reg, Bb.reg], writes=[pA])
                    P.op("act", lambda e, pB=pB: e.activation(out=Bb.ap[0:C, :, 0:C],
                                                              in_=pB.ap.rearrange("p (h t) -> p h t", t=C), func=AF.Copy),
                         reads=[pB], writes=[Bb.reg])
                    pX = bank(4 * C, parts=C)

                    def mm(e, pX=pX):
                        ins = None
                        for h in range(4):
                            ins = e.matmul(pX.ap[:, h * C:(h + 1) * C], Bh(h), Xh(h), start=True, stop=True)
                        return ins
                    P.op("pe", mm, reads=[Bb.reg, Xb.reg], writes=[pX])
                    if needA:
                        P.op("dve", lambda e, pA=pA: e.tensor_copy(out=Ab.ap[0:C, :, 0:C],
                                                                   in_=pA.ap.rearrange("p (h t) -> p h t", t=C)),
                             reads=[pA], writes=[Ab.reg])
                    P.op("dve", lambda e, pX=pX: e.tensor_tensor(out=Xb.ap[0:C, :, 0:C], in0=Xb.ap[0:C, :, 0:C],
                                                                 in1=pX.ap.rearrange("p (h t) -> p h t", t=C), op=ALU.add),
                         reads=[pX, Xb.reg], writes=[Xb.reg])
                XF = lambda h: Xb.ap[0:C, h, 0:C]
                if BST < 12:
                    return
                pk = bank(512, parts=C)
                pv = bank(512, parts=C)

                def mm(e):
                    ins = None
                    for h in range(4):
                        e.transpose(pk.ap[:, h * 128:(h + 1) * 128], QKV.ap[:, 4 + h, 0:C], cst.ap[:, IDENT, :])
                        ins = e.transpose(pv.ap[:, h * 128:(h + 1) * 128], QKV.ap[:, 8 + h, 0:C], cst.ap[:, IDENT, :])
                    return ins
                P.op("pe", mm, reads=[QKV.reg, cst], writes=[pk, pv])
                for (dstb, pp, col) in ((KBG, pk, cbg), (KDEC, pk, ckd), (VB, pv, cb)):
                    P.op("dve", lambda e, dstb=dstb, pp=pp, col=col: e.tensor_tensor(
                        out=dstb.ap[0:C, :, :], in0=pp.ap.rearrange("p (h d) -> p h d", d=128),
                        in1=col.unsqueeze(2).broadcast_to([C, 4, 128]), op=ALU.mult),
                        reads=[pp, COLS.reg], writes=[dstb.reg])
                pw = bank(4 * C)

                def mm(e):
                    ins = None
                    for h in range(4):
                        ins = e.matmul(pw.ap[:, h * C:(h + 1) * C], KBG.ap[0:C, h, :], XF(h), start=True, stop=True)
                    return ins
                P.op("pe", mm, reads=[KBG.reg, Xb.reg], writes=[pw])
                P.op("act", lambda e: e.activation(out=WNT.ap[:, :, 0:C], in_=pw.ap.rearrange("p (h t) -> p h t", t=C),
                                                   func=AF.Copy, scale=-1.0), reads=[pw], writes=[WNT.reg])
                P.op("act", lambda e: e.activation(out=DLb.ap[:, :, 0:C], in_=GCROW.ap[:, :, 0:C], func=AF.Exp, bias=lnsc),
                     reads=[GCROW.reg, COLS.reg], writes=[DLb.reg])
                P.op("dve", lambda e: e.tensor_tensor(out=QDT.ap[:, :, 0:C], in0=DLb.ap[:, :, 0:C], in1=QKV.ap[:, 0:4, 0:C],
                                                      op=ALU.mult), reads=[DLb.reg, QKV.reg], writes=[QDT.reg])
                if BST < 14:
                    return
                nb = 2 if aug else 1
                hpb = 4 // nb
                pvn = [bank(hpb * ZW, parts=C) for _ in range(nb)]

                def mm(e):
                    ins = None
                    for h in range(4):
                        o = pvn[h // hpb].ap[:, (h % hpb) * ZW:(h % hpb + 1) * ZW]
                        e.matmul(o[:, SO:SO + 128], XF(h), VB.ap[0:C, h, :], start=True, stop=False)
                        ins = e.matmul(o[:, SO:SO + 128], WNT.ap[:, h, 0:C], ZB.ap[:, h, SO:SO + 128], start=False, stop=True)
                        if aug:
                            ins = e.matmul(o[:, 0:128], WNT.ap[:, h, 0:C], ZB.ap[:, h, 0:128], start=True, stop=True)
                    return ins
                P.op("pe", mm, reads=[Xb.reg, VB.reg, WNT.reg, ZB.reg], writes=pvn)
                for i2 in range(nb):
                    P.op("act" if i2 == 0 else "dve",
                         (lambda e, i2=i2: e.activation(out=VN.ap[0:C, i2 * hpb:(i2 + 1) * hpb, 0:ZW],
                                                        in_=pvn[i2].ap.rearrange("p (h t) -> p h t", t=ZW), func=AF.Copy))
                         if i2 == 0 else
                         (lambda e, i2=i2: e.tensor_copy(out=VN.ap[0:C, i2 * hpb:(i2 + 1) * hpb, 0:ZW],
                                                         in_=pvn[i2].ap.rearrange("p (h t) -> p h t", t=ZW))),
                         reads=[pvn[i2]], writes=[VN.reg])
                po = bank(4 * C)
                pr = bank(4 * C) if aug else None

                def mm(e):
                    ins = None
                    for h in range(4):
                        e.matmul(po.ap[:, h * C:(h + 1) * C], ZB.ap[:, h, SO:SO + 128], QDT.ap[:, h, 0:C], start=True, stop=False)
                        ins = e.matmul(po.ap[:, h * C:(h + 1) * C], VN.ap[0:C, h, SO:SO + 128], QKMT.ap[0:C, h, 0:C],
                                       start=False, stop=True)
                        if aug:
                            e.matmul(pr.ap[:, h * C:(h + 1) * C], ZB.ap[:, h, 0:128], QDT.ap[:, h, 0:C], start=True, stop=False)
                            ins = e.matmul(pr.ap[:, h * C:(h + 1) * C], VN.ap[0:C, h, 0:128], QKMT.ap[0:C, h, 0:C],
                                           start=False, stop=True)
                    return ins
                P.op("pe", mm, reads=[ZB.reg, QDT.reg, VN.reg, QKMT.reg], writes=[po] + ([pr] if aug else []))
                oreg = sb.view(OZT.lo + a * 2, [3 * NT + C], BF16)
                P.op("act", lambda e: e.activation(out=OZT.ap[:, 0:4, a:a + C], in_=po.ap.rearrange("p (h t) -> p h t", t=C),
                                                   func=AF.Copy), reads=[po], writes=[oreg])
                if aug:
                    rreg = sb.view(OZT.lo + (4 * NT + a) * 2, [3 * NT + C], BF16)
                    P.op("dve", lambda e: e.tensor_copy(out=OZT.ap[:, 4:8, a:a + C],
                                                        in_=pr.ap.rearrange("p (h t) -> p h t", t=C)),
                         reads=[pr], writes=[rreg])
                pz = [bank(hpb * ZW) for _ in range(nb)]

                def mm(e):
                    ins = None
                    for h in range(4):
                        o = pz[h // hpb].ap[:, (h % hpb) * ZW:(h % hpb + 1) * ZW]
                        ins = e.matmul(o, KDEC.ap[0:C, h, :], VN.ap[0:C, h, 0:ZW], start=True, stop=True)
                    return ins
                P.op("pe", mm, reads=[KDEC.reg, VN.reg], writes=pz)
                for h in range(4):
                    o = pz[h // hpb].ap[:, (h % hpb) * ZW:(h % hpb + 1) * ZW]
                    P.op("dve", lambda e, h=h, o=o: e.scalar_tensor_tensor(
                        out=Zb.ap[:, h, 0:ZW], in0=Zb.ap[:, h, 0:ZW], scalar=gam[:, h:h + 1], in1=o,
                        op0=ALU.mult, op1=ALU.add), reads=[Zb.reg, COLS.reg, pz[h // hpb]], writes=[Zb.reg])
                P.op("act", lambda e: e.activation(out=ZB.ap[:, :, 0:ZW], in_=Zb.ap[:, :, 0:ZW], func=AF.Copy),
                     reads=[Zb.reg], writes=[ZB.reg])

            P.op("dve", lambda e: e.memset(ONES1.ap, 1.0), writes=[ONES1.reg])
            for si in range(2):
                P.dma("sp", "sld", lambda e, si=si: [e.dma_start(out=Zb.ap[:, :, 0:128],
                                                                  in_=sdelta_d[l, si].rearrange("h p d -> p h d"))],
                      writes=[Zb.reg])
                P.op("act", lambda e: e.activation(out=ZB.ap[:, :, 0:128], in_=Zb.ap[:, :, 0:128], func=AF.Copy),
                     reads=[Zb.reg], writes=[ZB.reg])
                P.dma("sp", "shl", lambda e, si=si: [e.dma_start(out=HALO.ap, in_=sconv_d[:, l, si])], writes=[HALO.reg])
                block(NP + si * 16, 16, 128, 0, si)
                P.dma("sp", "osd", lambda e, si=si: [e.dma_start(out=o_sdelta[l, si].rearrange("h p d -> p h d"),
                                                                  in_=Zb.ap[:, :, 0:128])], reads=[Zb.reg])
            P.op("dve", lambda e: e.tensor_copy(out=HALO.ap, in_=HSEL.ap[:, 0:12, HW - 3:HW]), reads=[HSEL.reg], writes=[HALO.reg])
            P.op("dve", lambda e: e.memset(Zb.ap, 0.0), writes=[Zb.reg])
            for h in range(4):
                P.op("dve", lambda e, h=h: e.tensor_copy(out=Zb.ap[:, h, 0:128], in_=cst.ap[:, IDENT, :]),
                     reads=[cst], writes=[Zb.reg])
            P.op("act", lambda e: e.activation(out=ZB.ap, in_=Zb.ap, func=AF.Copy), reads=[Zb.reg], writes=[ZB.reg])
            nblk = cfg.get("nblk", NBLK)
            for bi in range(nblk):
                block(bi * 128, 128, 256, 128, None)
            ptp = bank(512)

            def mm(e):
                ins = None
                for h in range(4):
                    ins = e.transpose(ptp.ap[:, h * 128:(h + 1) * 128], Zb.ap[:, h, 0:128], cst.ap[:, IDENT, :])
                return ins
            P.op("pe", mm, reads=[Zb.reg, cst], writes=[ptp])
            P.op("act", lambda e: e.activation(out=Zb.ap[:, :, 0:128], in_=ptp.ap.rearrange("p (h t) -> p h t", t=128),
                                               func=AF.Copy), reads=[ptp], writes=[Zb.reg])
            SACC = sb.view(PRE.lo, [4, 128], F32)
            ZR = sb.view(PRE.lo + 2048, [4, 256], F32)
            SNEW = sb.view(PRE.lo + 2048 + 4096, [4, 128], F32)
            SBF = sb.view(PRE.lo + 2048 + 4096 + 2048, [4, 128], BF16)
            assert PRE.lo + 2048 + 4096 + 2048 + 1024 <= KBG.lo
            P.op("dve", lambda e: e.memset(SACC.ap, 0.0), writes=[SACC])
            if XCH:
                rb_in = Reg(f"ccb_in{l}", 0, 1, None)
                rb_out = Reg(f"ccb_out{l}", 0, 1, None)
                P.dma("sp", "xb_st", lambda e: [e.dma_start(out=ccb_in[l][:, :].rearrange("p (h t) -> p h t", t=256), in_=Zb.ap)],
                      reads=[Zb.reg], writes=[rb_in])
                P.dma("pool", "xb_cc", lambda e: [e.collective_compute(
                    "AllGather", ALU.bypass, replica_groups=GROUPS, ins=[ccb_in[l].ap().opt()], outs=[ccb_out[l].ap().opt()])],
                    reads=[rb_in], writes=[rb_out], inc=1)
                for r in range(3):
                    P.dma("sp", "xb_ld", lambda e, r=r: [e.dma_start(
                        out=ZR.ap, in_=ccb_out[l][r * 128:(r + 1) * 128, :].rearrange("p (h t) -> p h t", t=256))],
                        reads=[rb_out], writes=[ZR])
                    pf = bank(512)

                    def mm(e, pf=pf):
                        ins = None
                        for h in range(4):
                            ins = e.matmul(pf.ap[:, h * 128:(h + 1) * 128], ZR.ap[:, h, 0:128], SACC.ap[:, h, :],
                                           start=True, stop=True)
                        return ins
                    P.op("pe", mm, reads=[ZR, SACC], writes=[pf])
                    P.op("dve", lambda e, pf=pf: e.tensor_tensor(out=SNEW.ap, in0=pf.ap.rearrange("p (h t) -> p h t", t=128),
                                                                 in1=ZR.ap[:, :, 128:256], op=ALU.add),
                         reads=[pf, ZR], writes=[SNEW])
                    P.op("dve", lambda e: e.tensor_tensor(out=SNEW.ap, in0=SNEW.ap, in1=SACC.ap, op=ALU.subtract),
                         reads=[SNEW, SACC], writes=[SNEW])
                    P.op("dve", lambda e, r=r: e.scalar_tensor_tensor(out=SACC.ap, in0=SNEW.ap, scalar=perc.ap[:, 4 + r:5 + r],
                                                                      in1=SACC.ap, op0=ALU.mult, op1=ALU.add),
                         reads=[SNEW, SACC, perc], writes=[SACC])
            P.op("act", lambda e: e.activation(out=SBF.ap, in_=SACC.ap, func=AF.Copy), reads=[SACC], writes=[SBF])
            pfin = bank(512)

            def mm(e):
                ins = None
                for h in range(4):
                    ins = e.matmul(pfin.ap[:, h * 128:(h + 1) * 128], Zb.ap[:, h, 0:128], SACC.ap[:, h, :], start=True, stop=True)
                return ins
            P.op("pe", mm, reads=[Zb.reg, SACC], writes=[pfin])
            P.op("dve", lambda e: e.tensor_tensor(out=SNEW.ap, in0=pfin.ap.rearrange("p (h t) -> p h t", t=128),
                                                  in1=Zb.ap[:, :, 128:256], op=ALU.add), reads=[pfin, Zb.reg], writes=[SNEW])
            P.dma("sp", "opd", lambda e: [e.dma_start(out=o_pdelta[l].rearrange("h p d -> p h d"), in_=SNEW.ap)], reads=[SNEW])
            if STG < 3:
                return
            pm = [PRE.lo + 9216]

            def palloc(nbytes):
                lo = pm[0]
                pm[0] += (nbytes + 31) // 32 * 32
                assert pm[0] <= SH1, (pm[0], SH1)
                return lo
            T1 = [palloc(512 * 4) for _ in range(2)]
            T2 = [palloc(512 * 2) for _ in range(2)]
            T3 = [palloc(512 * 2) for _ in range(2)]
            ones128 = sb.view(palloc(128 * 2), [128], BF16)
            P.op("dve", lambda e: e.memset(ones128.ap, 1.0 / 128.0), writes=[ones128])
            ctr = 0
            for h in range(4):
                wgt = wtile(wgate_d[l][h])
                for (a, n) in TT:
                    ctr += 1
                    o = sb.view(OZT.lo + (h * NT + a) * 2, [n], BF16)
                    t1 = sb.view(T1[ctr % 2], [n], F32)
                    t2 = sb.view(T2[ctr % 2], [n], BF16)
                    if a < NP and XCH:
                        pc = bank(n)
                        rt = sb.view(OZT.lo + ((4 + h) * NT + a) * 2, [n], BF16)
                        P.op("pe", lambda e, pc=pc, rt=rt, h=h: e.matmul(pc.ap, SBF.ap[:, h, :], rt.ap, start=True, stop=True),
                             reads=[SBF, rt], writes=[pc])
                        P.op("dve", lambda e, pc=pc, o=o: e.tensor_tensor(out=o.ap, in0=pc.ap, in1=o.ap, op=ALU.add),
                             reads=[pc, o], writes=[o])
                    P.op("dve", lambda e, o=o, t2=t2: e.tensor_tensor(out=t2.ap, in0=o.ap, in1=o.ap, op=ALU.mult),
                         reads=[o], writes=[t2])
                    pss = bank(n)
                    P.op("pe", lambda e, pss=pss, t2=t2: e.matmul(pss.ap, ones128.ap, t2.ap, start=True, stop=True),
                         reads=[ones128, t2], writes=[pss])
                    P.op("act", lambda e, pss=pss, t1=t1: e.activation(out=t1.ap, in_=pss.ap, func=AF.Ln, bias=eps_c.ap),
                         reads=[pss, eps_c], writes=[t1])
                    P.op("act", lambda e, t1=t1: e.activation(out=t1.ap, in_=t1.ap, func=AF.Exp, scale=-0.5),
                         reads=[t1], writes=[t1])
                    P.op("dve", lambda e, o=o, t1=t1: e.scalar_tensor_tensor(out=o.ap, in0=o.ap, scalar=onorm, in1=t1.ap,
                                                                            op0=ALU.mult, op1=ALU.mult),
                         reads=[o, t1, small], writes=[o])
                for (a, n) in TT:
                    ctr += 1
                    o = sb.view(OZT.lo + (h * NT + a) * 2, [n], BF16)
                    t3 = sb.view(T3[ctr % 2], [n], BF16)
                    pg = proj(wgt, xn, a, n)
                    P.op("act", lambda e, pg=pg, t3=t3: e.activation(out=t3.ap, in_=pg.ap, func=AF.Silu),
                         reads=[pg], writes=[t3])
                    P.op("dve", lambda e, o=o, t3=t3: e.tensor_tensor(out=o.ap, in0=o.ap, in1=t3.ap, op=ALU.mult),
                         reads=[o, t3], writes=[o])
            if STG < 4:
                return
            pm[0] = post0
            UP = sb.view(palloc(UPW * 4), [UPW], F32)
            SA = sb.view(palloc(UPW * 4), [UPW], F32)
            SBb = sb.view(palloc(UPW * 4), [UPW], F32)
            DB = sb.view(palloc(NT * 2), [NT], BF16)
            wpl = sb.view(palloc(4 * 128 * 2), [4, 128], BF16)
            t16 = sb.view(palloc(64), [16], F32)
            P.dma("pool", "wpl", lambda e: [e.dma_start(out=wpl.ap, in_=wpool_d[l].rearrange("g c d -> c g d"))],
                  writes=[wpl])
            def ucol(a):
                if a < NP:
                    return HW + a
                s_ = (a - NP) // 16
                return HW + NP + s_ * (HW + 16) + HW + (a - NP - s_ * 16)
            for g in range(4):
                win = 2 << g
                wu_t = wtile(wpu_d[l][g])
                P.op("dve", lambda e, g=g: e.tensor_copy(out=UP.ap[:, 0:HW], in_=HSEL.ap[:, 12 + g, :]), reads=[HSEL.reg], writes=[UP])
                for si in range(2):
                    c0 = HW + NP + si * (HW + 16)
                    P.dma("sp", "sph", lambda e, si=si, c0=c0, g=g: [e.dma_start(out=UP.ap[:, c0:c0 + HW],
                                                                                 in_=spool_d[:, l, si, g, :])], writes=[UP])
                for (a, n) in TT:
                    pb = proj(wu_t, xn, a, n)
                    if a < NP:
                        P.op("act", lambda e, pb=pb, a=a, n=n: e.activation(out=UP.ap[:, HW + a:HW + a + n], in_=pb.ap,
                                                                            func=AF.Copy), reads=[pb], writes=[UP])
                    else:
                        for si in range(2):
                            c0 = ucol(NP + si * 16)
                            P.op("act", lambda e, pb=pb, si=si, c0=c0: e.activation(
                                out=UP.ap[:, c0:c0 + 16], in_=pb.ap[:, si * 16:(si + 1) * 16], func=AF.Copy),
                                reads=[pb], writes=[UP])
                for si in range(2):
                    c0 = ucol(NP + si * 16)
                    P.dma("sp", "osp", lambda e, si=si, c0=c0, g=g: [e.dma_start(out=o_spool[l, si, :, g, :],
                                                                                 in_=UP.ap[:, c0:c0 + 16])], reads=[UP])
                src = UP
                bufs = [SA, SBb]
                step = 1
                k = 0
                while step < win:
                    dst = bufs[k % 2]
                    P.op("dve", lambda e, src=src, dst=dst, step=step: e.tensor_tensor(
                        out=dst.ap[:, step:UPW], in0=src.ap[:, step:UPW], in1=src.ap[:, 0:UPW - step], op=ALU.add),
                        reads=[src], writes=[dst])
                    P.op("dve", lambda e, src=src, dst=dst, step=step: e.tensor_copy(out=dst.ap[:, 0:step], in_=src.ap[:, 0:step]),
                         reads=[src], writes=[dst])
                    src = dst
                    step *= 2
                    k += 1
                for (a, n) in TT:
                    if a < NP:
                        segs = [(a, n, HW + a)]
                    else:
                        segs = [(NP + si * 16, 16, ucol(NP + si * 16)) for si in range(2)]
                    for (ta, tn, uc) in segs:
                        P.op("dve", lambda e, src=src, ta=ta, tn=tn, uc=uc, win=win: e.scalar_tensor_tensor(
                            out=DB.ap[:, ta:ta + tn], in0=src.ap[:, uc:uc + tn], scalar=1.0 / win, in1=UP.ap[:, uc:uc + tn],
                            op0=ALU.mult, op1=ALU.subtract), reads=[src, UP], writes=[DB])
                P.op("dve", lambda e, src=src, g=g: e.tensor_tensor(out=t16.ap, in0=src.ap[:, HW:HW + 16],
                                                                     in1=perc.ap[:, 8 + g * 16:8 + (g + 1) * 16], op=ALU.mult),
                     reads=[src, perc], writes=[t16])
                P.op("dve", lambda e: e.tensor_tensor(out=DB.ap[:, 0:16], in0=t16.ap, in1=UP.ap[:, HW:HW + 16], op=ALU.subtract),
                     reads=[t16, UP], writes=[DB])
                for (a, n) in TT:
                    pb = bank(n)
                    d = sb.view(DB.lo + a * 2, [n], BF16)
                    P.op("pe", lambda e, pb=pb, d=d, g=g: e.matmul(pb.ap, wpl.ap[:, g, :], d.ap, start=True, stop=True),
                         reads=[wpl, d], writes=[pb])
                    z = sb.view(OZT.lo + ((4 + g) * NT + a) * 2, [n], BF16)
                    P.op("dve", lambda e, pb=pb, z=z, g=g: e.tensor_scalar(out=z.ap, in0=pb.ap, scalar1=pscale(g), scalar2=None,
                                                                           op0=ALU.mult), reads=[pb, small], writes=[z])
            if DBG == "oz" and l == 0:
                ozs = sb.view(OZT.lo, [8, NT], BF16)
                P.dma("pool", "dbg", lambda e: [e.dma_start(out=dbg_d[:, 0:256].rearrange("p (c t) -> p c t", t=32),
                                                             in_=OZT.ap[:, :, NP:NT])], reads=[ozs])
            if STG < 5:
                return
            for dc in range(KC):
                wo = wtile(wout_d[l][dc])
                for (a, n) in TT:
                    pb = proj(wo, lambda c, a_, n_: sb.view(OZT.lo + (c * NT + a_) * 2, [n_], BF16), a, n)
                    x = xT(dc, a, n)
                    P.op("dve", lambda e, x=x, pb=pb: e.tensor_tensor(out=x.ap, in0=pb.ap, in1=x.ap, op=ALU.add),
                         reads=[pb, x], writes=[x])

        for l in range(NL):
            if not cfg.get("noffn"):
                ffn(l, 0)
            mixer(l)
            if not cfg.get("noffn"):
                ffn(l, 1)
        rmsnorm(DEPTH * 3)
        for c in range(KC):
            for (a, n) in TT:
                x = xT(c, a, n)
                w = nrmw(DEPTH * 3, c)
                r = rstd(a, n)
                P.op("dve", lambda e, x=x, w=w, r=r: e.scalar_tensor_tensor(
                    out=x.ap, in0=x.ap, scalar=w.ap, in1=r.ap, op0=ALU.mult, op1=ALU.mult),
                    reads=[x, w, r], writes=[x])
        P.dma("sp", "yout", lambda e: [e.dma_start(out=yT_d.rearrange("c p t -> p c t"), in_=xT_all.ap)],
              reads=[xT_all])
        P.wait_all_dma("sp")

        sems = {}
        for i, k in enumerate(P.semkeys):
            sems[k] = es.enter_context(nc.semaphore(f"s{i}"))
        block_ = es.enter_context(nc.Block())

        def run(engobj, name):
            for waits, fn, inc in P.streams[name]:
                for k, v in waits:
                    engobj.wait_ge(sems[k], v)
                if fn is None:
                    continue
                r = fn(engobj)
                if isinstance(r, list):
                    for ins in r:
                        ins.then_inc(sems[inc[0]], inc[1])
                else:
                    r.then_inc(sems[inc[0]], inc[1])

        @block_.tensor
        def _(e):
            run(e, "pe")

        @block_.scalar
        def _(e):
            run(e, "act")

        @block_.vector
        def _(e):
            run(e, "dve")

        @block_.gpsimd
        def _(e):
            run(e, "pool")

        @block_.sync
        def _(e):
            run(e, "sp")
    return nc


def _wtiles(w, col0, nch):
    sub = w[:, col0:col0 + nch * 128]
    return np.ascontiguousarray(sub.reshape(KC, 128, nch, 128).transpose(2, 1, 0, 3))


def _prep_inputs(inp):
    f = np.float32
    xp, xs = inp["x_prompt"], inp["x_sample"]
    nrm = np.zeros((128, DEPTH * 3 + 1, KC), f)
    for l in range(DEPTH):
        for i, nm in enumerate(("norm_ffn1", "norm_mix", "norm_ffn2")):
            nrm[:, l * 3 + i, :] = inp[nm][l].reshape(KC, 128).T
    nrm[:, DEPTH * 3, :] = inp["norm_final"].reshape(KC, 128).T
    shared = {"nrm": nrm}
    small = np.zeros((128, DEPTH, 72), f)
    for l in range(DEPTH):
        for fi, fn in enumerate(("ffn1", "ffn2")):
            shared[f"wg{l}{fi}"] = _wtiles(inp[f"w_{fn}_gate"][l], 0, FC)
            shared[f"wu{l}{fi}"] = _wtiles(inp[f"w_{fn}_up"][l], 0, FC)
            d = inp[f"w_{fn}_down"][l]
            shared[f"wd{l}{fi}"] = np.ascontiguousarray(d.reshape(2, 11, 128, KC, 128).transpose(0, 3, 2, 1, 4))
        wi = inp["w_in"][l]
        shared[f"wqkv{l}"] = _wtiles(wi, 0, 12)
        shared[f"wgate{l}"] = _wtiles(wi, 1536, 4)
        shared[f"wpu{l}"] = _wtiles(wi, 2056, 4)
        shared[f"wab{l}"] = np.ascontiguousarray(wi[:, 2048:2056].reshape(KC, 128, 8).transpose(1, 0, 2))
        shared[f"wout{l}"] = _wtiles(inp["w_out"][l], 0, KC)
        shared[f"wpool{l}"] = np.ascontiguousarray(inp["w_pool"][l])
        cwl = inp["conv_w"][l]
        small[:, l, 0:48] = cwl.reshape(4, 12, 128).transpose(2, 1, 0).reshape(128, 48)
        small[:, l, 48:52] = inp["a_log"][l][None, :]
        small[:, l, 52:56] = inp["dt_bias"][l][None, :]
        small[:, l, 56] = inp["o_norm"][l]
        small[:, l, 57:61] = inp["pool_scale"][l].reshape(4, 128).T
    shared["small"] = small
    cst = np.zeros((128, 5, 128), f)
    ii = np.arange(128)
    cst[:, 0, :] = np.eye(128)
    cst[:, 1, :] = (ii[:, None] <= ii[None, :])
    cst[:, 2, :] = (ii[None, :] > ii[:, None])
    cst[:, 3, :] = (ii[:, None] > ii[None, :])
    cst[:, 4, :] = (ii[None, :] >= ii[:, None]) * np.float32(128.0 ** -0.5)
    shared["consts"] = cst
    maps = []
    for c in range(NCORES):
        b, t = c // 4, c % 4
        tok = np.concatenate([xp[b, t * NP:(t + 1) * NP], xs[2 * c].reshape(16, D), xs[2 * c + 1].reshape(16, D)], 0)
        m = dict(shared)
        m["xT"] = np.ascontiguousarray(tok.T.reshape(KC, 128, NT))
        m["sdelta"] = np.ascontiguousarray(inp["state_delta"][:, 2 * c:2 * c + 2])
        sc = inp["state_conv"][:, 2 * c:2 * c + 2]
        m["sconv"] = np.ascontiguousarray(sc.reshape(DEPTH, 2, 3, 12, 128).transpose(4, 0, 1, 3, 2))
        sp = inp["state_pool"][:, 2 * c:2 * c + 2]
        spl = np.zeros((128, DEPTH, 2, 4, HW), f)
        spl[..., 1:] = sp.reshape(DEPTH, 2, 15, 4, 128).transpose(4, 0, 1, 3, 2)
        m["spool"] = spl
        pc = np.zeros((128, 72), f)
        if t > 0:
            pc[:, t - 1] = 1.0
        for r in range(3):
            pc[:, 4 + r] = 1.0 if r < t else 0.0
        for g in range(4):
            win = 2 << g
            pos = t * NP + np.arange(16)
            pc[:, 8 + g * 16:8 + (g + 1) * 16] = (1.0 / np.minimum(win, pos + 1))[None, :]
        m["percore"] = pc
        maps.append(m)
    return maps


_NC_CACHE = {}


def kernel(**inputs):
    import os
    import json
    inp = {k: np.asarray(v, dtype=np.float32) for k, v in inputs.items()}
    maps = _prep_inputs(inp)
    cfg = json.loads(os.environ.get("KCFG", "{}"))
    if "nc" not in _NC_CACHE:
        _NC_CACHE["nc"] = build_nc(cfg=cfg)
    nc = _NC_CACHE["nc"]
    res = run_bass_kernel_spmd(nc, maps, core_ids=list(range(NCORES)))
    R = res.results
    if cfg.get("dbg"):
        _NC_CACHE["dbg"] = [R[c]["dbg"] for c in range(NCORES)]
    y_p = np.zeros((2, 8192, D), np.float32)
    y_s = np.zeros((16, 16, D), np.float32)
    p_delta = np.zeros((DEPTH, 2, 4, 128, 128), np.float32)
    p_conv = np.zeros((DEPTH, 2, 3, 1536), np.float32)
    p_pool = np.zeros((DEPTH, 2, 15, 512), np.float32)
    s_delta = np.zeros((DEPTH, 16, 4, 128, 128), np.float32)
    s_conv = np.zeros((DEPTH, 16, 3, 1536), np.float32)
    s_pool = np.zeros((DEPTH, 16, 15, 512), np.float32)
    for c in range(NCORES):
        b, t = c // 4, c % 4
        y = R[c]["yT"].reshape(D, NT).T
        y_p[b, t * NP:(t + 1) * NP] = y[:NP]
        y_s[2 * c] = y[NP:NP + 16]
        y_s[2 * c + 1] = y[NP + 16:NP + 32]
        if t == 3:
            p_delta[:, b] = R[c]["o_pdelta"]
            hl = R[c]["o_halo"]
            p_conv[:, b] = hl[:, :, 0:12, HW - 3:].transpose(0, 3, 2, 1).reshape(DEPTH, 3, 1536)
            p_pool[:, b] = hl[:, :, 12:16, 1:].transpose(0, 3, 2, 1).reshape(DEPTH, 15, 512)
        s_delta[:, 2 * c:2 * c + 2] = R[c]["o_sdelta"]
        sc = R[c]["o_sconv"]
        s_conv[:, 2 * c:2 * c + 2] = sc.transpose(0, 1, 4, 3, 2).reshape(DEPTH, 2, 3, 1536)
        spo = R[c]["o_spool"]
        s_pool[:, 2 * c:2 * c + 2] = spo[..., 1:].transpose(0, 1, 4, 3, 2).reshape(DEPTH, 2, 15, 512)
    return (y_p, y_s, p_delta, p_conv, p_pool, s_delta, s_conv, s_pool)
```
